# Optimizing a Trainium2 kernel written in Bass

```python
import jax, jax.numpy as jnp
from jax import lax
import numpy as np

D_MODEL = 1024
BATCH = 4
SEQ = 8192
DEPTH = 1

PLE_DIM = 256
HEAD_DIM = 64
A_HEADS = 8
A_CONFIGS = ((128, 1), (512, 4), (2048, 16))
B_HEADS = 8
B_KV_GROUPS = 2
B_REP = B_HEADS // B_KV_GROUPS
CMP_LEN = 32
CMP_STRIDE = 16
CMP_HIDDEN = 256
SEL_LEN = 64
SEL_TOPK = 16
WIN = 512
Q_BLK = 128
D_FF = 4 * D_MODEL
N_BRANCHES = 2
A_WIDTH = A_HEADS * HEAD_DIM
B_WIDTH = B_HEADS * HEAD_DIM
KV_WIDTH = B_KV_GROUPS * HEAD_DIM
C_IN = 3 * A_WIDTH + B_WIDTH + 6 * KV_WIDTH + 3 * B_HEADS + N_BRANCHES * D_MODEL
EPS = 1e-6
NEG = -1e30
FORCE = 1e9

kernel_name = 'hybrid_dilated_nsa_gated_block'


def rmsnorm(x, g):
    xf = x.astype(jnp.float32)
    inv = lax.rsqrt(jnp.mean(xf * xf, axis=-1, keepdims=True) + EPS)
    return (xf * inv).astype(x.dtype) * g


def alibi_slopes(n):
    return jnp.asarray([2.0 ** (-8.0 * (h + 1) / n) for h in range(n)], jnp.float32)


def masked_softmax(s, mask):
    s = jnp.where(mask, s, NEG)
    m = jnp.max(s, axis=-1, keepdims=True)
    e = jnp.where(mask, jnp.exp(s - m), 0.0)
    denom = jnp.sum(e, axis=-1, keepdims=True)
    p = e / jnp.maximum(denom, 1e-30)
    lse = m[..., 0] + jnp.log(jnp.maximum(denom[..., 0], 1e-30))
    return p, lse


def split_input_projection(proj):
    sizes = [A_WIDTH] * 3 + [B_WIDTH] + [KV_WIDTH] * 6 + [3 * B_HEADS] + [D_MODEL] * N_BRANCHES
    points = [int(v) for v in np.cumsum(sizes)[:-1]]
    return jnp.split(proj, points, axis=-1)


def dilated_window_attention(q, k, v):
    B, S, H, Dh = q.shape
    scale = HEAD_DIM ** -0.5
    slopes = alibi_slopes(A_HEADS)[:, None, None, None]
    outs, lses = [], []
    for window, dil in A_CONFIGS:
        W = window // dil
        L = S // dil
        blk = min(Q_BLK, L)
        nblk = L // blk

        def strided(t):
            return t.reshape(B, L, dil, H, Dh).transpose(0, 2, 3, 1, 4)

        qs = strided(q).reshape(B, dil, H, nblk, blk, Dh)
        pad = ((0, 0), (0, 0), (0, 0), (W, 0), (0, 0))
        kp = jnp.pad(strided(k), pad)
        vp = jnp.pad(strided(v), pad)
        idx = jnp.arange(nblk)[:, None] * blk + jnp.arange(blk + W)[None, :]
        kb = kp[:, :, :, idx]
        vb = vp[:, :, :, idx]
        i = jnp.arange(blk)[:, None]
        j = jnp.arange(blk + W)[None, :]
        dist = i - j + W
        key_pos = jnp.arange(nblk)[:, None, None] * blk + j[None] - W
        mask = (dist >= 0) & (dist <= W) & (key_pos >= 0)
        s = jnp.einsum('brhnqd,brhnkd->brhnqk', qs, kb).astype(jnp.float32) * scale
        s = s - slopes * (dist * dil).astype(jnp.float32)
        p, lse = masked_softmax(s, mask)
        o = jnp.einsum('brhnqk,brhnkd->brhnqd', p.astype(vb.dtype), vb)
        outs.append(o.reshape(B, dil, H, L, Dh).transpose(0, 3, 1, 2, 4).reshape(B, S, H, Dh))
        lses.append(lse.reshape(B, dil, H, L).transpose(0, 3, 1, 2).reshape(B, S, H))
    w = jax.nn.softmax(jnp.stack(lses, axis=-1), axis=-1)
    return jnp.einsum('bshc,cbshd->bshd', w.astype(q.dtype), jnp.stack(outs, axis=0))


def compress_blocks(t, cidx, pe, w1, w2):
    B, G, _, Dh = t.shape
    blocks = t[:, :, cidx] + pe
    flat = blocks.reshape(B, G, cidx.shape[0], CMP_LEN * Dh)
    return jax.nn.gelu(flat @ w1) @ w2


def selection_overlap(ncmp, nsel):
    ratio = SEL_LEN // CMP_STRIDE
    span = CMP_LEN // CMP_STRIDE
    i = np.arange(ncmp)[:, None]
    j = np.arange(nsel)[None, :]
    ov = np.minimum(i + span, ratio * (j + 1)) - np.maximum(i, ratio * j)
    return np.maximum(ov, 0).astype(np.float32)


def native_sparse_attention(q, kc, vc, ks, vs, kw, vw, gates, pe_k, wk1, wk2, pe_v, wv1, wv2):
    B, S, _ = q.shape
    G, R, Dh = B_KV_GROUPS, B_REP, HEAD_DIM
    scale = HEAD_DIM ** -0.5
    q = q.reshape(B, S, G, R, Dh).transpose(0, 2, 3, 1, 4)

    def kv(t):
        return t.reshape(B, S, G, Dh).transpose(0, 2, 1, 3)

    kc, vc, ks, vs, kw, vw = kv(kc), kv(vc), kv(ks), kv(vs), kv(kw), kv(vw)
    ncmp = (S - CMP_LEN) // CMP_STRIDE + 1
    cidx = jnp.arange(ncmp)[:, None] * CMP_STRIDE + jnp.arange(CMP_LEN)[None, :]
    cmp_end = cidx[:, -1]
    k_cmp = compress_blocks(kc, cidx, pe_k, wk1, wk2)
    v_cmp = compress_blocks(vc, cidx, pe_v, wv1, wv2)
    nsel = S // SEL_LEN
    overlap = jnp.asarray(selection_overlap(ncmp, nsel))
    topk = min(SEL_TOPK, nsel)
    ks_blk = ks.reshape(B, G, nsel, SEL_LEN, Dh)
    vs_blk = vs.reshape(B, G, nsel, SEL_LEN, Dh)
    kw_pad = jnp.pad(kw, ((0, 0), (0, 0), (WIN, 0), (0, 0)))
    vw_pad = jnp.pad(vw, ((0, 0), (0, 0), (WIN, 0), (0, 0)))
    slopes = alibi_slopes(B_HEADS).reshape(G, R)[None, :, :, None, None]
    blk_id = jnp.arange(nsel)
    bi = jnp.arange(B)[:, None, None, None]
    gi = jnp.arange(G)[None, :, None, None]

    def one_block(q0):
        qb = lax.dynamic_slice_in_dim(q, q0, Q_BLK, axis=3)
        t = q0 + jnp.arange(Q_BLK)
        dist_c = t[:, None] - cmp_end[None, :]
        s = jnp.einsum('bgrqd,bgnd->bgrqn', qb, k_cmp).astype(jnp.float32) * scale
        s = s - slopes * dist_c.astype(jnp.float32)
        p_cmp, _ = masked_softmax(s, dist_c >= 0)
        o_cmp = jnp.einsum('bgrqn,bgnd->bgrqd', p_cmp.astype(v_cmp.dtype), v_cmp)
        imp = jnp.einsum('bgrqn,nj->bgqj', p_cmp, overlap)
        cur = t // SEL_LEN
        forced = (blk_id[None, :] == 0) | (blk_id[None, :] == cur[:, None]) | (blk_id[None, :] == cur[:, None] - 1)
        allowed = blk_id[None, :] * SEL_LEN <= t[:, None]
        rank = jnp.where(allowed, imp + jnp.where(forced, FORCE, 0.0), NEG)
        _, sel = lax.top_k(rank, topk)
        kg = ks_blk[bi, gi, sel].reshape(B, G, Q_BLK, topk * SEL_LEN, Dh)
        vg = vs_blk[bi, gi, sel].reshape(B, G, Q_BLK, topk * SEL_LEN, Dh)
        pos = (sel[..., None] * SEL_LEN + jnp.arange(SEL_LEN)).reshape(B, G, Q_BLK, topk * SEL_LEN)
        dist_s = (t[None, None, :, None] - pos)[:, :, None]
        s = jnp.einsum('bgrqd,bgqkd->bgrqk', qb, kg).astype(jnp.float32) * scale
        s = s - slopes * dist_s.astype(jnp.float32)
        p_slc, _ = masked_softmax(s, dist_s >= 0)
        o_slc = jnp.einsum('bgrqk,bgqkd->bgrqd', p_slc.astype(vg.dtype), vg)
        kwb = lax.dynamic_slice_in_dim(kw_pad, q0, Q_BLK + WIN, axis=2)
        vwb = lax.dynamic_slice_in_dim(vw_pad, q0, Q_BLK + WIN, axis=2)
        kpos = q0 - WIN + jnp.arange(Q_BLK + WIN)
        dist_w = t[:, None] - kpos[None, :]
        mask_w = (dist_w >= 0) & (dist_w < WIN) & (kpos[None, :] >= 0)
        s = jnp.einsum('bgrqd,bgkd->bgrqk', qb, kwb).astype(jnp.float32) * scale
        s = s - slopes * dist_w.astype(jnp.float32)
        p_win, _ = masked_softmax(s, mask_w)
        o_win = jnp.einsum('bgrqk,bgkd->bgrqd', p_win.astype(vwb.dtype), vwb)
        return jnp.stack([o_cmp, o_slc, o_win], axis=-2)

    n_qblk = S // Q_BLK
    out = lax.map(one_block, jnp.arange(n_qblk) * Q_BLK)
    out = out.transpose(1, 0, 4, 2, 3, 5, 6).reshape(B, S, B_HEADS, 3, Dh)
    g = jax.nn.sigmoid(gates.reshape(B, S, B_HEADS, 3))
    return jnp.einsum('bshc,bshcd->bshd', g.astype(out.dtype), out)


def setup_inputs(seed: int = 0) -> dict:
    key = jax.random.key(seed)
    ks = jax.random.split(key, 20)
    f32 = jnp.float32

    def nrm(k, shape, fan_in):
        return jax.random.normal(k, shape, f32) * (fan_in ** -0.5)

    def gain(k, shape):
        return 1.0 + 0.02 * jax.random.normal(k, shape, f32)

    return {
        'x': jax.random.normal(ks[0], (BATCH, SEQ, D_MODEL), f32),
        'p': jax.random.normal(ks[1], (DEPTH, BATCH, SEQ, PLE_DIM), f32),
        'norm_mix_g': gain(ks[2], (DEPTH, D_MODEL)),
        'w_in': nrm(ks[3], (DEPTH, D_MODEL, C_IN), D_MODEL),
        'pe_ck': 0.1 * jax.random.normal(ks[4], (DEPTH, CMP_LEN, HEAD_DIM), f32),
        'w_ck1': nrm(ks[5], (DEPTH, CMP_LEN * HEAD_DIM, CMP_HIDDEN), CMP_LEN * HEAD_DIM),
        'w_ck2': nrm(ks[6], (DEPTH, CMP_HIDDEN, HEAD_DIM), CMP_HIDDEN),
        'pe_cv': 0.1 * jax.random.normal(ks[7], (DEPTH, CMP_LEN, HEAD_DIM), f32),
        'w_cv1': nrm(ks[8], (DEPTH, CMP_LEN * HEAD_DIM, CMP_HIDDEN), CMP_LEN * HEAD_DIM),
        'w_cv2': nrm(ks[9], (DEPTH, CMP_HIDDEN, HEAD_DIM), CMP_HIDDEN),
        'w_up_a': nrm(ks[10], (DEPTH, A_WIDTH, D_MODEL), A_WIDTH),
        'w_up_b': nrm(ks[11], (DEPTH, B_WIDTH, D_MODEL), B_WIDTH),
        'w_out': nrm(ks[12], (DEPTH, D_MODEL, D_MODEL), D_MODEL),
        'norm_mlp_g': gain(ks[13], (DEPTH, D_MODEL)),
        'w_mlp1': nrm(ks[14], (DEPTH, D_MODEL, D_FF), D_MODEL),
        'w_mlp2': nrm(ks[15], (DEPTH, D_FF, D_MODEL), D_FF),
        'norm_ple_g': gain(ks[16], (DEPTH, D_MODEL)),
        'w_ple_gate': nrm(ks[17], (DEPTH, D_MODEL, D_MODEL), D_MODEL),
        'w_ple': nrm(ks[18], (DEPTH, PLE_DIM, D_MODEL), PLE_DIM),
        'norm_final_g': gain(ks[19], (D_MODEL,)),
    }


def reference(x, p, norm_mix_g, w_in, pe_ck, w_ck1, w_ck2, pe_cv, w_cv1, w_cv2,
              w_up_a, w_up_b, w_out, norm_mlp_g, w_mlp1, w_mlp2,
              norm_ple_g, w_ple_gate, w_ple, norm_final_g):
    B, S, _ = x.shape
    h = x
    for i in range(DEPTH):
        n = rmsnorm(h, norm_mix_g[i])
        (qa, ka, va, qb, kc, vc, ksl, vsl, kwn, vwn,
         nsa_gates, gate_a, gate_b) = split_input_projection(n @ w_in[i])
        o_a = dilated_window_attention(qa.reshape(B, S, A_HEADS, HEAD_DIM),
                                       ka.reshape(B, S, A_HEADS, HEAD_DIM),
                                       va.reshape(B, S, A_HEADS, HEAD_DIM))
        o_b = native_sparse_attention(qb, kc, vc, ksl, vsl, kwn, vwn, nsa_gates,
                                      pe_ck[i], w_ck1[i], w_ck2[i], pe_cv[i], w_cv1[i], w_cv2[i])
        y_a = o_a.reshape(B, S, A_WIDTH) @ w_up_a[i]
        y_b = o_b.reshape(B, S, B_WIDTH) @ w_up_b[i]
        mixed = jax.nn.sigmoid(gate_a) * y_a + jax.nn.sigmoid(gate_b) * y_b
        h = h + mixed @ w_out[i]
        n2 = rmsnorm(h, norm_mlp_g[i])
        h = h + jnp.square(jax.nn.relu(n2 @ w_mlp1[i])) @ w_mlp2[i]
        gate = jax.nn.sigmoid(rmsnorm(h, norm_ple_g[i]) @ w_ple_gate[i])
        h = h + gate * (p[i] @ w_ple[i])
    return rmsnorm(h, norm_final_g)
```

```python
import contextlib
import numpy as np
import concourse.bass as bass
import concourse.mybir as mybir
from concourse.bass_utils import run_bass_kernel_spmd

F32 = mybir.dt.float32
BF16 = mybir.dt.bfloat16
AF = mybir.ActivationFunctionType
ALU = mybir.AluOpType

S = 8192
SO = 4096
D = 1024
NEGM = -30000.0
SLOPES = [2.0 ** (-(h + 1)) for h in range(8)]
EPS = 1e-6
SEM_LIMIT = 30000
NDMA = 20


class Tok:
    __slots__ = ("es", "sem", "val")

    def __init__(self, es=None, sem=None, val=None):
        self.es, self.sem, self.val = es, sem, val


class Buf:
    __slots__ = ("w", "r", "name")

    def __init__(self, name=""):
        self.w = None
        self.r = {}
        self.name = name


class EngState:
    def __init__(self, name, eng):
        self.name, self.eng = name, eng
        self.sem = None
        self.cnt = 0
        self.last = None
        self.pending = []
        self.waited = {}


class _PEProxy:
    def __init__(self, eng):
        self.eng = eng
        self.stop = False

    def matmul(self, *a, **k):
        self.stop = bool(k.get("stop"))
        return self.eng.matmul(*a, **k)

    def transpose(self, *a, **k):
        self.stop = True
        return self.eng.transpose(*a, **k)


class Sched:
    def __init__(self, nc):
        self.nc = nc
        self.E = {
            "pe": EngState("pe", nc.tensor),
            "act": EngState("act", nc.scalar),
            "dve": EngState("dve", nc.vector),
            "pool": EngState("pool", nc.gpsimd),
            "sp": EngState("sp", nc.sync),
        }
        self.nsem = 0
        self.dma_sems = [self._sem() for _ in range(NDMA)]
        self.dma_cnt = [0] * NDMA
        self.dma_last = [None] * NDMA
        self.dma_rr = {"sp": 0, "pool": 0, "act": 0}
        self.uid = 0
        self.out_toks = []

    def _sem(self):
        self.nsem += 1
        return self.nc.alloc_semaphore("s%d" % self.nsem)

    def resolve(self, tok):
        if tok.val is not None:
            return
        es = tok.es
        if es.sem is None or es.cnt >= SEM_LIMIT:
            es.sem = self._sem()
            es.cnt = 0
        es.cnt += 1
        es.last.then_inc(es.sem, 1)
        for t in es.pending:
            t.sem, t.val = es.sem, es.cnt
        es.pending = []

    def wait(self, es, tok):
        if tok is None:
            return
        self.resolve(tok)
        k = id(tok.sem)
        if es.waited.get(k, 0) >= tok.val:
            return
        es.eng.wait_ge(tok.sem, tok.val)
        es.waited[k] = tok.val

    def _deps(self, es, reads, writes, is_dma):
        pe = (not is_dma) and es.name == "pe"
        for b in reads:
            t = b.w
            if t is None or (pe and t.es is es):
                continue
            self.wait(es, t)
        for b in writes:
            t = b.w
            transitive = t is not None and t.es is es and any(r.es is not es for r in b.r.values())
            if t is not None and not (pe and t.es is es) and not transitive:
                self.wait(es, t)
            for k, t in b.r.items():
                if not (pe and t.es is es):
                    self.wait(es, t)

    def op(self, q, fn, reads=(), writes=(), mark=None):
        es = self.E[q]
        self._deps(es, reads, writes, False)
        if q == "pe":
            px = _PEProxy(es.eng)
            ins = fn(px)
            eager = px.stop if mark is None else mark
        else:
            ins = fn(es.eng)
            eager = True if mark is None else mark
        es.last = ins
        tok = Tok(es=es)
        es.pending.append(tok)
        es.last_tok = tok
        if eager:
            self.resolve(tok)
        for b in reads:
            b.r[q] = tok
        for b in writes:
            b.w = tok
            b.r = {}
        return tok

    def dma(self, q, out, in_, reads=(), writes=(), is_output=False, **kw):
        es = self.E[q]
        self._deps(es, reads, writes, True)
        half = NDMA // 2
        k = self.dma_rr[q]
        self.dma_rr[q] = (k + 1) % half
        i = k + (half if q == "pool" else 0)
        if self.dma_last[i] is not None:
            self.wait(es, self.dma_last[i])
        self.dma_cnt[i] += 16
        if q == "pool":
            kw.setdefault("max_dma_last_dim", 4096)
        ins = es.eng.dma_start(out=out, in_=in_, **kw)
        ins.then_inc(self.dma_sems[i], 16)
        tok = Tok(es=None, sem=self.dma_sems[i], val=self.dma_cnt[i])
        self.dma_last[i] = tok
        self.uid += 1
        for b in reads:
            b.r["dma%d" % self.uid] = tok
        for b in writes:
            b.w = tok
            b.r = {}
        if is_output:
            self.out_toks.append(tok)
        return tok

    def barrier(self):
        toks = []
        for es in self.E.values():
            if getattr(es, "last_tok", None) is not None:
                toks.append(es.last_tok)
        for t in toks:
            self.resolve(t)
        alltoks = toks + [t for t in self.dma_last if t is not None]
        for es in self.E.values():
            for t in alltoks:
                if t.es is es:
                    continue
                self.wait(es, t)

    def fence(self, q):
        es = self.E[q]
        if getattr(es, "last_tok", None) is not None:
            self.wait(es, es.last_tok)

    def finish(self):
        es = self.E["sp"]
        for t in self.dma_last:
            if t is not None:
                self.wait(es, t)


class Ring:
    def __init__(self, items):
        self.items = items
        self.i = -1

    def next(self):
        self.i = (self.i + 1) % len(self.items)
        return self.items[self.i]


def pipeline(n, stages):
    ns = len(stages)
    for t in range(n + ns - 1):
        for s in range(ns):
            i = t - s
            if 0 <= i < n:
                stages[s](i)


class Builder:
    def __init__(self, debug=None):
        self.debug = debug or {}
        nc = bass.Bass("TRN2", target_bir_lowering=False)
        self.nc = nc
        self.sc = Sched(nc)
        self.din = {}
        self.psum_banks = [nc.alloc_psum_tensor("bank%d" % i, [128, 512], F32) for i in range(8)]
        self.gstack = contextlib.ExitStack()
        self.Pbank = [Buf() for _ in range(8)]
        self.PbankD = [Buf() for _ in range(2)]
        self.pstack = None
        self.nalloc = 0

    def inp(self, name, shape, dt=F32):
        t = self.nc.dram_tensor(name, list(shape), dt, kind="ExternalInput").ap()
        self.din[name] = t
        return t

    def scratch(self, name, shape, dt):
        kind = "ExternalOutput" if name in self.debug else "Internal"
        return self.nc.dram_tensor(name, list(shape), dt, kind=kind).ap()

    def sb(self, name, shape, dt, persistent=False):
        st = self.gstack if (persistent or self.pstack is None) else self.pstack
        self.nalloc += 1
        t = st.enter_context(self.nc.sbuf_tensor("%s_%d" % (name, self.nalloc), list(shape), dt))
        return t.ap()

    def begin_phase(self):
        self.pstack = contextlib.ExitStack()

    def end_phase(self):
        self.sc.barrier()
        self.pstack.close()
        self.pstack = None

    def bank(self, i, dt=F32):
        t = self.psum_banks[i]
        if dt is F32:
            return t.ap()
        return t.bitcast(dt).ap() if hasattr(t, "bitcast") else t.ap()


    def setup_consts(self):
        sc = self.sc
        self.ident = self.sb("ident", [128, 128], BF16, True)
        self.Bident = Buf()
        sc.dma("pool", self.ident, self.inp("ident", [128, 128]), writes=[self.Bident])
        self.ones32 = self.sb("ones32", [128, 128], F32, True)
        self.Bones32 = Buf()
        sc.op("pool", lambda e: e.memset(self.ones32, 1.0), writes=[self.Bones32])
        self.onesh = self.sb("onesh", [128, 2, 128], BF16, True)
        self.Bonesh = Buf()
        sc.op("pool", lambda e: e.memset(self.onesh, 0.0), writes=[self.Bonesh])
        sc.op("pool", lambda e: e.memset(self.onesh[:, 0, 0:64], 1.0), writes=[self.Bonesh])
        sc.op("pool", lambda e: e.memset(self.onesh[:, 1, 64:128], 1.0), writes=[self.Bonesh])
        self.zeros = self.sb("zeros", [128, 512], BF16, True)
        self.Bzeros = Buf()
        sc.op("pool", lambda e: e.memset(self.zeros, 0.0), writes=[self.Bzeros])
        self.kvb = self.sb("kvb", [128, 8], F32, True)
        self.Bkvb = Buf()
        sc.dma("sp", self.kvb, self.inp("kvb", [128, 8]), writes=[self.Bkvb])
        self.epsc = self.sb("epsc", [128, 1], F32, True)
        self.Beps = Buf()
        sc.op("pool", lambda e: e.memset(self.epsc, EPS), writes=[self.Beps])
        self.gains = self.sb("gains", [128, 4, 8], F32, True)
        self.Bgains = Buf()
        sc.dma("sp", self.gains, self.inp("gains", [128, 4, 8]), writes=[self.Bgains])

    def rmsnorm_tile(self, x_ap, Bx, out_ap, Bout, gi, T, tmp, ps_bank):
        sc = self.sc
        sq, Bsq, rs, Brs, ri, Bri = tmp
        for h0 in range(0, T, 512):
            ps = self.bank(ps_bank)
            Pps = self.Pbank[ps_bank]
            sc.op("act", lambda e: e.activation(out=sq[:, :, 0:512], in_=x_ap[:, :, h0:h0 + 512], func=AF.Square),
                  reads=[Bx], writes=[Bsq])
            for n_ in (4, 2, 1):
                sc.op("dve", lambda e, n_=n_: e.tensor_tensor(out=sq[:, 0:n_, 0:512], in0=sq[:, 0:n_, 0:512],
                                                              in1=sq[:, n_:2 * n_, 0:512], op=ALU.add),
                      reads=[Bsq], writes=[Bsq])
            sc.op("pe", lambda e: e.matmul(ps, lhsT=self.ones32, rhs=sq[:, 0, 0:512], start=True, stop=True),
                  reads=[self.Bones32, Bsq], writes=[Pps])
            sc.op("act", lambda e: e.activation(out=rs, in_=ps, func=AF.Sqrt, scale=1.0 / D, bias=self.epsc[:, 0:1]),
                  reads=[Pps, self.Beps], writes=[Brs])
            sc.op("dve", lambda e: e.reciprocal(out=ri, in_=rs), reads=[Brs], writes=[Bri])
            for c in range(8):
                sc.op("dve", lambda e, c=c: e.scalar_tensor_tensor(
                    out=out_ap[:, c, h0:h0 + 512], in0=x_ap[:, c, h0:h0 + 512], scalar=self.gains[:, gi, c:c + 1],
                    in1=ri, op0=ALU.mult, op1=ALU.mult),
                    reads=[Bx, self.Bgains, Bri], writes=[Bout])

    def norm_tmp(self):
        return (self.sb("nsq", [128, 8, 512], F32), Buf(), self.sb("nrs", [128, 512], F32), Buf(),
                self.sb("nri", [128, 512], F32), Buf())

    def phase0(self):
        sc = self.sc
        self.begin_phase()
        xT = self.inp("xT", [D, S])
        self.nT = self.scratch("nT", [D, S], BF16)
        self.BnT = [Buf() for _ in range(16)]
        xr = xT.rearrange("(c p) t -> p c t", p=128)
        self.xr = xr
        nr = self.nT.rearrange("(c p) t -> p c t", p=128)
        self.nr = nr
        xt = [(self.sb("p0x", [128, 8, 512], F32), Buf()) for i in range(2)]
        nt = [(self.sb("p0n", [128, 8, 512], BF16), Buf()) for i in range(2)]
        tmp = self.norm_tmp()

        def load(j):
            a, b = xt[j % 2]
            sc.dma("sp", a, xr[:, :, j * 512:(j + 1) * 512], writes=[b])

        load(0)
        for j in range(16):
            if j + 1 < 16:
                load(j + 1)
            a, b = xt[j % 2]
            na, nb = nt[j % 2]
            self.rmsnorm_tile(a, b, na, nb, 0, 512, tmp, 7)
            sc.dma("pool", nr[:, :, j * 512:(j + 1) * 512], na, reads=[nb], writes=[self.BnT[j]])
        self.end_phase()

    A_CFG = [(1, 128, 0), (4, 64, 512), (16, 64, 768)]

    def q_own_ap(self, t, dil, r, mt):
        if dil == 1:
            ob = (mt - 1) // 2
            return t[:, ob * 128:(ob + 1) * 128]
        if dil == 4:
            return t[:, 256 * mt:256 * mt + 256].rearrange("p (a x) -> p a x", a=2)[:, :, r::4]
        return t[:, 1024 * mt:1024 * mt + 1024].rearrange("p (a x) -> p a x", a=8)[:, :, r::16]

    @staticmethod
    def v3(ap, dil):
        if dil == 1:
            return ap
        return ap.rearrange("p (a x) -> p a x", a=(2 if dil == 4 else 8))

    def phaseA(self, hps=(0, 1, 2, 3)):
        sc = self.sc
        self.begin_phase()
        wA = self.inp("wA", [4, 128, 8, 384])
        abias = self.inp("abias", [4, 128, 1024])
        self.oaT = self.scratch("oaT", [512, SO], BF16)
        self.BoaT = Buf()
        WA, BWA = self.sb("WA", [128, 8, 384], BF16), Buf()
        AB, BAB = self.sb("AB", [128, 1024], BF16), Buf()
        KT = self.sb("KT", [128, S], BF16)
        VT = self.sb("VT", [128, S], BF16)
        BKT = [Buf() for _ in range(16)]
        BVT = [Buf() for _ in range(16)]
        QTZ = self.sb("QTZ", [128, 2, SO], BF16)
        QT = [QTZ[:, 0, :], QTZ[:, 1, :]]
        BQT = [[Buf() for _ in range(16)] for _ in range(2)]
        onesf, Bonesf = self.sb("onesf", [128, 128], BF16), Buf()
        sc.op("pool", lambda e: e.memset(onesf, 1.0), writes=[Bonesf])
        accn, accd = self.sb("accn", [128, SO], F32), self.sb("accd", [128, SO], F32)
        Baccn, Baccd = Buf(), Buf()
        ntr = [(self.sb("ntA", [128, 8, 512], BF16), Buf()) for _ in range(2)]
        Vt = [[(self.sb("Vt", [128, 128], BF16), Buf()) for _ in range(4)] for _ in range(2)]
        Er = [(self.sb("EA", [128, 256], BF16), Buf()) for _ in range(3)]
        ostg = [(self.sb("oastg", [128, 1024], BF16), Buf()) for _ in range(2)]
        sc.op("pool", lambda e: e.memset(QT[0][64:128, :], 0.0), writes=[b for b in BQT[0]])
        sc.op("pool", lambda e: e.memset(QT[1][0:64, :], 0.0), writes=[b for b in BQT[1]])
        for i in range(4):
            sc.op("pool", lambda e, i=i: e.memset(Vt[0][i][0][:, 64:128], 0.0), writes=[Vt[0][i][1]])
            sc.op("pool", lambda e, i=i: e.memset(Vt[1][i][0][:, 0:64], 0.0), writes=[Vt[1][i][1]])
        TB = self.debug.get("a_tb", [6, 7, 6, 7])
        psT = [self.bank(b).bitcast(BF16)[:, 0:128] for b in TB]
        if self.debug.get("a_init"):
            sc.op("dve", lambda e: e.memset(accn, 0.0), writes=[Baccn])
            sc.op("dve", lambda e: e.memset(accd, 1.0), writes=[Baccd])
        PT = [self.Pbank[b] for b in TB]
        for hp in hps:
            sc.dma("pool", WA, wA[hp], writes=[BWA])
            sc.dma("pool", AB, abias[hp], writes=[BAB])
            self.cast_some(2)
            pi = 0
            for j in range(16):
                na, nb = ntr[j % 2]
                sc.dma("sp", na, self.nr[:, :, j * 512:(j + 1) * 512], reads=[self.BnT[j]], writes=[nb])
                for which in range(3):
                    bi = pi % 2
                    pi += 1
                    ps, Pps = self.bank(bi), self.Pbank[bi]
                    if which == 0:
                        rhs = lambda c: na[:, c, :].rearrange("p (a x) -> p a x", a=2)[:, :, 128:256]
                        outp = ps[:, 0:256].rearrange("p (a x) -> p a x", a=2)
                    else:
                        rhs = lambda c: na[:, c, :]
                        outp = ps
                    for c in range(8):
                        sc.op("pe", lambda e, c=c: e.matmul(outp, lhsT=WA[:, c, which * 128:(which + 1) * 128], rhs=rhs(c),
                                                            start=(c == 0), stop=(c == 7)),
                              reads=[BWA, nb], writes=[Pps])
                    if which == 0:
                        sc.op("act", lambda e: e.copy(out=QT[0][0:64, j * 256:(j + 1) * 256], in_=ps[0:64, 0:256]),
                              reads=[Pps], writes=[BQT[0][j]])
                        sc.op("dve", lambda e: e.tensor_copy(out=QT[1][64:128, j * 256:(j + 1) * 256], in_=ps[64:128, 0:256]),
                              reads=[Pps], writes=[BQT[1][j]])
                    elif which == 1:
                        sc.op("act", lambda e: e.copy(out=KT[:, j * 512:(j + 1) * 512], in_=ps), reads=[Pps], writes=[BKT[j]])
                    else:
                        sc.op("dve", lambda e: e.tensor_copy(out=VT[:, j * 512:(j + 1) * 512], in_=ps), reads=[Pps], writes=[BVT[j]])
            steps = []
            for ci, (dil, Nq, boff) in enumerate(self.A_CFG):
                nmt = 64 // dil
                for r in range(dil):
                    for mt in range(nmt):
                        has_q = (mt % 2 == 1) if dil == 1 else True
                        kts = ([mt - 1] if mt >= 1 else []) + [mt]
                        if not has_q:
                            steps.append(dict(tr_only=True, ci=ci, dil=dil, r=r, kt=mt))
                            continue
                        for k_i, kt in enumerate(kts):
                            steps.append(dict(tr_only=False, ci=ci, dil=dil, Nq=Nq, boff=boff, r=r, mt=mt, kt=kt,
                                              kind=(0 if kt == mt else 1), first=(k_i == 0), last=(k_i == len(kts) - 1),
                                              do_tr=(kt == mt)))
            ring_state = dict(s=0, e=0, n=0)
            for st_ in steps:
                st_["sslot"] = None

            def tile_bufs(dil, r, kt):
                lo = 128 * dil * kt + r
                hi = lo + dil * 127
                return lo, list(range(lo // 512, hi // 512 + 1))

            def stA(i):
                st = steps[i]
                dil, r, kt = st["dil"], st["r"], st["kt"]
                lo, tb = tile_bufs(dil, r, kt)
                if (st["tr_only"] or st["do_tr"]) and not (self.debug.get("a_notr2") and i == 2):
                    slot = kt % 4
                    src = VT[:, lo:lo + dil * 127 + 1:dil]
                    sc.op("pe", lambda e: e.transpose(psT[slot], src, self.ident),
                          reads=[BVT[x] for x in tb] + [self.Bident], writes=[PT[slot]])
                    sc.op("dve", lambda e: e.tensor_copy(out=Vt[0][slot][0][:, 0:64], in_=psT[slot][:, 0:64]),
                          reads=[PT[slot]], writes=[Vt[0][slot][1]])
                    sc.op(self.debug.get("a_ev2", "dve"), lambda e: (e.copy if self.debug.get("a_ev2", "dve") == "act" else e.tensor_copy)(out=Vt[1][slot][0][:, 64:128], in_=psT[slot][:, 64:128]),
                          reads=[PT[slot]], writes=[Vt[1][slot][1]])
                if st["tr_only"] or (self.debug.get("a_nos2") and i == 2):
                    return
                Nq, boff, mt, kind = st["Nq"], st["boff"], st["mt"], st["kind"]
                ss = ring_state["s"] % 2
                ring_state["s"] += 1
                st["ss"] = ss
                ps_s = self.bank(2 + ss)[:, 0:2 * Nq]
                Pss = self.Pbank[2 + ss]
                bcol = boff + kind * 2 * Nq
                sc.op("pe", lambda e: e.matmul(ps_s, lhsT=self.ident, rhs=AB[:, bcol:bcol + 2 * Nq], start=True, stop=False),
                      reads=[self.Bident, BAB], writes=[Pss])
                kcols = KT[:, lo:lo + dil * 127 + 1:dil]
                if dil == 1:
                    ob_ = (mt - 1) // 2
                    qap = QTZ[:, :, ob_ * 128:(ob_ + 1) * 128]
                    oap = ps_s.rearrange("p (h x) -> p h x", h=2)
                    qb = [BQT[0][(mt - 1) // 4], BQT[1][(mt - 1) // 4]]
                elif dil == 4:
                    qap = QTZ[:, :, 256 * mt:256 * mt + 256].rearrange("p h (a x) -> p h a x", a=2)[:, :, :, r::4]
                    oap = ps_s.rearrange("p (h a x) -> p h a x", h=2, a=2)
                    qb = [BQT[0][mt], BQT[1][mt]]
                else:
                    qap = QTZ[:, :, 1024 * mt:1024 * mt + 1024].rearrange("p h (a x) -> p h a x", a=8)[:, :, :, r::16]
                    oap = ps_s.rearrange("p (h a x) -> p h a x", h=2, a=8)
                    qb = [BQT[h][4 * mt + x] for h in range(2) for x in range(4)]
                sc.op("pe", lambda e: e.matmul(oap, lhsT=kcols, rhs=qap, start=False, stop=True),
                      reads=[BKT[x] for x in tb] + qb, writes=[Pss])

            def stB(i):
                st = steps[i]
                if st["tr_only"]:
                    return
                Nq = st["Nq"]
                ps_s = self.bank(2 + st["ss"])[:, 0:2 * Nq]
                es = ring_state["e"] % 3
                ring_state["e"] += 1
                st["es"] = es
                E, BE = Er[es]
                if st["kt"] == 0:
                    ci = st["ci"]
                    sc.op("act", lambda e: e.activation(out=E[:, 0:2 * Nq], in_=ps_s, func=AF.Exp, scale=0.125,
                                                        bias=self.kvb[:, ci:ci + 1]),
                          reads=[self.Pbank[2 + st["ss"]], self.Bkvb], writes=[BE])
                else:
                    sc.op("act", lambda e: e.activation(out=E[:, 0:2 * Nq], in_=ps_s, func=AF.Exp, scale=0.125),
                          reads=[self.Pbank[2 + st["ss"]]], writes=[BE])

            def stC(i):
                st = steps[i]
                if st["tr_only"]:
                    return
                Nq, dil, r, mt = st["Nq"], st["dil"], st["r"], st["mt"]
                E, BE = Er[st["es"]]
                if st["first"]:
                    ring_state["n"] += 1
                nsl = ring_state["n"] % 2
                slot = st["kt"] % 4
                ps_n, Pn = self.bank(4 + nsl)[:, 0:Nq], self.Pbank[4 + nsl]
                ps_d, Pd = self.bank(nsl)[:, 0:2 * Nq], self.Pbank[nsl]
                for h in range(2):
                    sc.op("pe", lambda e, h=h: e.matmul(ps_n, lhsT=Vt[h][slot][0], rhs=E[:, h * Nq:(h + 1) * Nq],
                                                        start=(st["first"] and h == 0), stop=(st["last"] and h == 1)),
                          reads=[Vt[h][slot][1], BE], writes=[Pn])
                sc.op("pe", lambda e: e.matmul(ps_d, lhsT=onesf, rhs=E[:, 0:2 * Nq], start=st["first"], stop=st["last"]),
                      reads=[Bonesf, BE], writes=[Pd])
                if st["last"]:
                    an = self.q_own_ap(accn, dil, r, mt)
                    ad0 = self.q_own_ap(accd[0:64, :], dil, r, mt)
                    ad1 = self.q_own_ap(accd[64:128, :], dil, r, mt)
                    pd0, pd1 = self.v3(ps_d[0:64, 0:Nq], dil), self.v3(ps_d[64:128, Nq:2 * Nq], dil)
                    if st["ci"] == 0:
                        sc.op("dve", lambda e: e.tensor_copy(out=an, in_=self.v3(ps_n, dil)), reads=[Pn], writes=[Baccn])
                        sc.op("dve", lambda e: e.tensor_copy(out=ad0, in_=pd0), reads=[Pd], writes=[Baccd])
                        sc.op("dve", lambda e: e.tensor_copy(out=ad1, in_=pd1), reads=[Pd], writes=[Baccd])
                    else:
                        sc.op("dve", lambda e: e.tensor_tensor(out=an, in0=an, in1=self.v3(ps_n, dil), op=ALU.add), reads=[Pn, Baccn], writes=[Baccn])
                        sc.op("dve", lambda e: e.tensor_tensor(out=ad0, in0=ad0, in1=pd0, op=ALU.add), reads=[Pd, Baccd], writes=[Baccd])
                        sc.op("dve", lambda e: e.tensor_tensor(out=ad1, in0=ad1, in1=pd1, op=ALU.add), reads=[Pd, Baccd], writes=[Baccd])

            if "a_nsteps" in self.debug:
                steps = steps[:self.debug["a_nsteps"]]
            if self.debug.get("a_stages", 3) == 3:
                pipeline(len(steps), [stA, stB, stC])
            elif self.debug.get("a_stages") == 2:
                pipeline(len(steps), [stA, stB])
            else:
                pipeline(len(steps), [stA])
            Bfin = Buf()
            sc.fence("dve")
            for q4 in range(4):
                sl = slice(q4 * 1024, (q4 + 1) * 1024)
                sc.op("dve", lambda e: e.reciprocal(out=accd[:, sl], in_=accd[:, sl]), reads=[Baccd], writes=[Baccd])
                og, Bog = ostg[q4 % 2]
                sc.op("dve", lambda e: e.tensor_tensor(out=og, in0=accn[:, sl], in1=accd[:, sl], op=ALU.mult),
                      reads=[Baccd, Baccn], writes=[Bog])
                sc.dma("sp", self.oaT[hp * 128:(hp + 1) * 128, sl], og, reads=[Bog], writes=[self.BoaT])
        self.end_phase()


def _a_bias_tables():
    out = np.zeros((4, 128, 1024), np.float32)
    j = np.arange(128)[:, None]
    for hp in range(4):
        for (dil, Nq, boff) in Builder.A_CFG:
            if dil == 1:
                u = np.arange(128)
            elif dil == 4:
                u = (32 + 64 * np.arange(2)[:, None] + np.arange(32)[None, :]).reshape(-1)
            else:
                u = (8 + 16 * np.arange(8)[:, None] + np.arange(8)[None, :]).reshape(-1)
            u = u[None, :]
            for kind in range(2):
                dist = (u - j) if kind == 0 else (128 + u - j)
                valid = (dist >= 0) & (dist <= 128)
                for h in range(2):
                    sl = SLOPES[2 * hp + h]
                    val = np.where(valid, -8.0 * sl * dil * dist, NEGM).astype(np.float32)
                    c0 = boff + kind * 2 * Nq + h * Nq
                    out[hp, :, c0:c0 + Nq] = val
    return out


def _consts_common():
    c = {}
    c["ident"] = np.eye(128, dtype=np.float32)
    c["abias"] = _a_bias_tables()
    t = np.arange(S)
    c["kaug"] = np.stack([np.ones(S), np.ones(S), 128.0 * (t // 128), (t % 128).astype(np.float64)], 0).astype(np.float32)
    pos = 16 * np.arange(512) + 31
    c["caug"] = np.stack([np.ones(512), np.ones(512), 128.0 * (pos // 128), (pos % 128).astype(np.float64)], 0).astype(np.float32)
    qaug = np.zeros((2, 4, 32, 4, 128), np.float32)
    for g in range(2):
        for h in range(4):
            s = SLOPES[4 * g + h]
            qaug[g, 0, :, h, :] = (-8.0 * s * 128.0 * (2 * np.arange(32) + 1))[:, None]
            qaug[g, 1, :, h, :] = (-8.0 * s * np.arange(128))[None, :]
            qaug[g, 2, :, h, :] = 8.0 * s
            qaug[g, 3, :, h, :] = 8.0 * s
    c["qaug"] = qaug.reshape(2, 4, 32 * 512)
    j = np.arange(128)[:, None, None]
    kt = np.arange(64)[None, :, None]
    k = np.arange(128)[None, None, :]
    c["rmat"] = np.where(j == 2 * kt + k // 64, 30000.0, 0.0).astype(np.float32).reshape(128, 64 * 128)
    jj = np.arange(128)[:, None, None]
    v = np.arange(8)[None, :, None]
    u = (np.arange(512) % 128)[None, None, :]
    c["cmask"] = np.where(16 * jj <= 256 * v + 97 + u, 0.0, NEGM).astype(np.float32).reshape(128, 8 * 512)
    j2 = np.arange(128)[:, None]
    u2 = (np.arange(512) % 128)[None, :]
    c["causal4"] = np.where(j2 <= u2, 0.0, NEGM).astype(np.float32)
    c["wmask4"] = np.where(j2 > u2, 0.0, NEGM).astype(np.float32)
    cc = np.arange(512)[:, None]
    jb = np.arange(128)[None, :]
    ov = np.maximum(np.minimum(cc + 2, 4 * (jb + 1)) - np.maximum(cc, 4 * jb), 0).astype(np.float32)
    ovx = np.zeros((512, 130), np.float32)
    ovx[:, 0:128] = ov
    ovx[:, 128] = 1.0
    c["ovx"] = np.ascontiguousarray(ovx.reshape(4, 128, 130).transpose(1, 0, 2)).reshape(128, 4 * 130)
    i = np.arange(128)[:, None]
    rel = np.arange(256)[None, :] - 126
    cur = (i >= 64).astype(np.int64)
    tt = np.where(rel > cur, -1e30, np.where((rel == cur) | (rel == cur - 1), 1e9, 0.0))
    c["ttab"] = tt.astype(np.float32)
    return c


def _pcadd(p):
    row = np.zeros(128, np.float32)
    if p == 1:
        row[0] = 1e9
    else:
        row[0:2] = -1e30
        row[2] = 1e9
    return np.tile(row[None, :], (128, 1))


def _kvb(p):
    kvb = np.zeros((128, 8), np.float32)
    if p == 0:
        kvb[:, 0] = NEGM
        kvb[0:32, 1] = NEGM
        kvb[0:8, 2] = NEGM
        kvb[0:8, 3] = NEGM
        kvb[:, 4] = NEGM
    return kvb


def _prep_weights(inputs):
    w = {}
    w_in = np.asarray(inputs["w_in"][0], np.float32)
    qa, ka, va = w_in[:, 0:512], w_in[:, 512:1024], w_in[:, 1024:1536]
    wA = np.zeros((4, 128, 8, 384), np.float32)
    for hp in range(4):
        cols = np.concatenate([qa[:, hp * 128:(hp + 1) * 128], ka[:, hp * 128:(hp + 1) * 128], va[:, hp * 128:(hp + 1) * 128]], 1)
        wA[hp] = cols.reshape(8, 128, 384).transpose(1, 0, 2)
    w["wA"] = wA
    qb = w_in[:, 1536:2048]
    wB = np.zeros((2, 128, 8, NCB), np.float32)
    for g in range(2):
        cols = np.zeros((D, NCB), np.float32)
        for h in range(4):
            cols[:, h * 128:h * 128 + 64] = qb[:, (4 * g + h) * 64:(4 * g + h + 1) * 64]
        kc = w_in[:, 2048 + 64 * g:2048 + 64 * g + 64]
        vc = w_in[:, 2176 + 64 * g:2176 + 64 * g + 64]
        cols[:, 512:576] = w_in[:, 2304 + 64 * g:2304 + 64 * g + 64]
        cols[:, 640:704] = w_in[:, 2560 + 64 * g:2560 + 64 * g + 64]
        cols[:, 768:832] = kc
        cols[:, 832:896] = kc
        cols[:, 896:960] = vc
        cols[:, 960:1024] = vc
        cols[:, 1024:1088] = w_in[:, 2432 + 64 * g:2432 + 64 * g + 64]
        cols[:, 1088:1152] = w_in[:, 2688 + 64 * g:2688 + 64 * g + 64]
        cols[:, 1152:1164] = w_in[:, 2816 + 12 * g:2816 + 12 * g + 12]
        wB[g] = cols.reshape(8, 128, NCB).transpose(1, 0, 2)
    w["wB"] = wB
    w["wcmp1"] = np.stack([np.asarray(inputs[k][0], np.float32).reshape(16, 128, 256).transpose(1, 0, 2) for k in ("w_ck1", "w_cv1")], 0)
    w["pecmp"] = np.stack([np.asarray(inputs[k][0], np.float32).reshape(16, 128).T for k in ("pe_ck", "pe_cv")], 0)
    w2 = np.zeros((2, 128, 2, 128), np.float32)
    for i, k in enumerate(("w_ck2", "w_cv2")):
        w2[i, :, :, 0:64] = np.asarray(inputs[k][0], np.float32).reshape(2, 128, 64).transpose(1, 0, 2)
    w["wcmp2"] = w2
    def chunk(W, co):
        K_ = W.shape[0]
        return np.ascontiguousarray(W[:, co * 128:(co + 1) * 128].reshape(K_ // 128, 128, 128).transpose(1, 0, 2)).reshape(-1)
    srcs = {"ga": w_in[:, 2840:3864], "gb": w_in[:, 3864:4888], "ua": np.asarray(inputs["w_up_a"][0], np.float32),
            "ub": np.asarray(inputs["w_up_b"][0], np.float32), "out": np.asarray(inputs["w_out"][0], np.float32),
            "w1": np.asarray(inputs["w_mlp1"][0], np.float32), "w2": np.asarray(inputs["w_mlp2"][0], np.float32),
            "pg": np.asarray(inputs["w_ple_gate"][0], np.float32), "ple": np.asarray(inputs["w_ple"][0], np.float32)}
    w["wtail"] = np.concatenate([chunk(srcs[nm], co) for (nm, co, kc) in TAIL_CH]).reshape(TAIL_N // 1024, 1024)
    gains = np.stack([inputs["norm_mix_g"][0], inputs["norm_mlp_g"][0], inputs["norm_ple_g"][0], inputs["norm_final_g"]], 0)
    w["gains"] = np.ascontiguousarray(np.asarray(gains, np.float32).reshape(4, 8, 128).transpose(2, 0, 1))
    return w


def _virtual_xT(xb, p):
    xt = np.ascontiguousarray(xb.T)
    if p == 1:
        return xt
    out = np.zeros_like(xt)
    out[:, 128:] = xt[:, :S - 128]
    return out


def build_program(debug=None, upto="all"):
    B = Builder(debug=debug)
    B.setup_consts()
    B.cast_tail_weights()
    B.phase0()
    if upto == "p0":
        return B
    if not (debug or {}).get("skipA"):
        B.phaseA(hps=(debug or {}).get("hps", (0, 1, 2, 3)))
    elif (debug or {}).get("ext_o"):
        B.oaT, B.BoaT = B.inp("oaT", [512, SO], BF16), Buf()
        B.obT, B.BobT = B.inp("obT", [512, SO], BF16), Buf()
    if upto == "A":
        return B
    if not (debug or {}).get("skipB"):
        B.phaseB(groups=(debug or {}).get("groups", (0, 1)))
    if upto == "B":
        return B
    B.cast_some(99)
    B.phaseT(tiles=(debug or {}).get("tiles", (0, 1, 2, 3)))
    return B


NCB = 1164


def _phaseB(self, groups=(0, 1)):
    sc = self.sc
    wB = self.inp("wB", [2, 128, 8, NCB])
    wcmp1 = self.inp("wcmp1", [2, 128, 16, 256])
    pecmp = self.inp("pecmp", [2, 128, 16])
    wcmp2 = self.inp("wcmp2", [2, 128, 2, 128])
    kaug = self.inp("kaug", [4, S])
    caug = self.inp("caug", [4, 512])
    qaug = self.inp("qaug", [2, 4, 32 * 512])
    rmat = self.inp("rmat", [128, 64 * 128])
    cmask = self.inp("cmask", [128, 8 * 512])
    causal4 = self.inp("causal4", [128, 512])
    wmask4 = self.inp("wmask4", [128, 512])
    ovx = self.inp("ovx", [128, 4 * 130])
    ttab = self.inp("ttab", [128, 256])
    pcadd = self.inp("pcadd", [128, 128])
    self.obT = self.scratch("obT", [512, SO], BF16)
    self.BobT = Buf()
    for g in groups:
        self.begin_phase()
        gstack = self.pstack
        QA = self.sb("QA", [68, 32 * 512], BF16)
        BQA = [Buf() for _ in range(16)]
        KS, KW = self.sb("KS", [68, S], BF16), self.sb("KW", [68, S], BF16)
        BKS, BKW = [Buf() for _ in range(16)], [Buf() for _ in range(16)]
        VS, VW = self.sb("VS", [128, 64, 66], BF16), self.sb("VW", [128, 64, 66], BF16)
        BVS, BVW = [Buf() for _ in range(64)], [Buf() for _ in range(64)]
        SG, BSG = self.sb("SG", [128, 32, 12], F32), Buf()
        KCMP, BKCMP = self.sb("KCMP", [68, 512], BF16), Buf()
        VCMP, BVCMP = self.sb("VCMP", [128, 4, 66], BF16), Buf()
        QA4 = QA.rearrange("p (n h q) -> p n h q", n=32, h=4)
        sc.dma("pool", KS[64:68, :], kaug, writes=BKS)
        sc.dma("pool", KW[64:68, :], kaug, writes=BKW)
        sc.dma("pool", QA[64:68, :], qaug[g], writes=BQA)
        sc.dma("pool", KCMP[64:68, :], caug, writes=[BKCMP])
        sc.op("pool", lambda e: e.memset(VS[:, :, 64:65], 1.0), writes=BVS)
        sc.op("pool", lambda e: e.memset(VW[:, :, 64:65], 1.0), writes=BVW)
        sc.op("pool", lambda e: e.memset(VCMP[:, :, 64:65], 1.0), writes=[BVCMP])
        self.pstack = contextlib.ExitStack()
        WB, BWB = self.sb("WB", [128, 8, NCB], BF16), Buf()
        sc.dma("pool", WB, wB[g], writes=[BWB])
        ntr = [(self.sb("ntB", [128, 8, 512], BF16), Buf()) for _ in range(2)]
        KC2 = [self.sb("KC2", [128, S + 32], BF16) for _ in range(2)]
        BKC = [[Buf() for _ in range(16)] for _ in range(2)]
        for kv in range(2):
            sc.op("pool", lambda e, kv=kv: e.memset(KC2[kv][:, S - 16:S + 32], 0.0), writes=[BKC[kv][15]])
        bk = [0]

        def nextbank():
            bk[0] = (bk[0] + 1) % 6
            return bk[0]

        ev = [0]

        def evac(out, in_, reads, writes, func=None):
            ev[0] += 1
            if func is not None:
                sc.op("act", lambda e: e.activation(out=out, in_=in_, func=func), reads=reads, writes=writes)
            elif ev[0] % 2 == 0:
                sc.op("act", lambda e: e.copy(out=out, in_=in_), reads=reads, writes=writes)
            else:
                sc.op("dve", lambda e: e.tensor_copy(out=out, in_=in_), reads=reads, writes=writes)

        bstop = self.debug.get("b_stop", 99)
        for j in range(16 if bstop >= 2 else 0):
            na, nb = ntr[j % 2]
            sc.dma("sp", na, self.nr[:, :, j * 512:(j + 1) * 512], reads=[self.BnT[j]], writes=[nb])
            own = lambda c: na[:, c, :].rearrange("p (a x) -> p a x", a=2)[:, :, 128:256]
            bskip = self.debug.get("b_skip", "")
            for h in range(0 if "q" in bskip else 4):
                b = nextbank()
                ps, Pps = self.bank(b), self.Pbank[b]
                for c in range(8):
                    sc.op("pe", lambda e, c=c: e.matmul(ps[:, 0:256].rearrange("p (a x) -> p a x", a=2),
                                                        lhsT=WB[:, c, h * 128:(h + 1) * 128], rhs=own(c), start=(c == 0), stop=(c == 7)),
                          reads=[BWB, nb], writes=[Pps])
                evac(QA4[0:64, 2 * j:2 * j + 2, h, :], ps[0:64, 0:256].rearrange("p (a x) -> p a x", a=2), [Pps], [BQA[j]])
            for which, (dst, Bd) in enumerate(((KS, BKS), (KW, BKW)) if "k" not in bskip else ()):
                b = nextbank()
                ps, Pps = self.bank(b), self.Pbank[b]
                c0 = 512 + which * 128
                for c in range(8):
                    sc.op("pe", lambda e, c=c: e.matmul(ps, lhsT=WB[:, c, c0:c0 + 128], rhs=na[:, c, :], start=(c == 0), stop=(c == 7)),
                          reads=[BWB, nb], writes=[Pps])
                evac(dst[0:64, j * 512:(j + 1) * 512], ps[0:64, :], [Pps], [Bd[j]])
            for kv in range(0 if "c" in bskip else 2):
                b = nextbank()
                ps, Pps = self.bank(b), self.Pbank[b]
                c0 = 768 + kv * 128
                for c in range(8):
                    sc.op("pe", lambda e, c=c: e.matmul(ps, lhsT=WB[:, c, c0:c0 + 128], rhs=na[:, c, :], start=(c == 0), stop=(c == 7)),
                          reads=[BWB, nb], writes=[Pps])
                evac(KC2[kv][0:64, j * 512:(j + 1) * 512], ps[0:64, :], [Pps], [BKC[kv][j]])
                if j == 0:
                    evac(KC2[kv][64:128, 0:511], ps[64:128, 1:512], [Pps], [BKC[kv][j]])
                else:
                    evac(KC2[kv][64:128, j * 512 - 1:j * 512 + 511], ps[64:128, :], [Pps], [BKC[kv][j], BKC[kv][j - 1]])
            for tb in range(0 if "t" in bskip else 4):
                b = nextbank()
                ps, Pps = self.bank(b), self.Pbank[b]
                for c in range(8):
                    sc.op("pe", lambda e, c=c: e.matmul(ps[:, 0:140], lhsT=na[:, c, tb * 128:(tb + 1) * 128], rhs=WB[:, c, 1024:1164],
                                                        start=(c == 0), stop=(c == 7)),
                          reads=[BWB, nb], writes=[Pps])
                kt = 4 * j + tb
                sc.op("dve", lambda e: e.tensor_copy(out=VS[:, kt, 0:64], in_=ps[:, 0:64]), reads=[Pps], writes=[BVS[kt]])
                sc.op("dve", lambda e: e.tensor_copy(out=VW[:, kt, 0:64], in_=ps[:, 64:128]), reads=[Pps], writes=[BVW[kt]])
                if tb % 2 == 1:
                    sc.op("dve", lambda e: e.tensor_copy(out=SG[:, 2 * j + tb // 2, :], in_=ps[:, 128:140]), reads=[Pps], writes=[BSG])
        if bstop >= 2:
            sc.op("act", lambda e: e.activation(out=SG, in_=SG, func=AF.Sigmoid), reads=[BSG], writes=[BSG])
        W1, BW1 = self.sb("W1c", [128, 16, 256], BF16), Buf()
        PEc, BPE = self.sb("PEc", [128, 16], BF16), Buf()
        W2, BW2 = self.sb("W2c", [128, 2, 128], BF16), Buf()
        b1sb, Bb1 = self.sb("b1sb", [128, 2], F32), Buf()
        X, BX = self.sb("cX", [128, 512], F32), Buf()
        X2, BX2 = self.sb("cX2", [128, 512], F32), Buf()
        TH, BTH = self.sb("cTH", [128, 512], F32), Buf()
        Gb = [(self.sb("cG", [128, 512], BF16), Buf()) for _ in range(2)]
        for kv in range(2 if bstop >= 3 else 0):
            sc.dma("pool", W1, wcmp1[kv], writes=[BW1])
            sc.dma("pool", PEc, pecmp[kv], writes=[BPE])
            sc.dma("pool", W2, wcmp2[kv], writes=[BW2])
            b = nextbank()
            psb, Ppsb = self.bank(b), self.Pbank[b]
            for hc in range(2):
                for l2 in range(16):
                    sc.op("pe", lambda e, l2=l2: e.matmul(psb[:, hc:hc + 1], lhsT=W1[:, l2, hc * 128:(hc + 1) * 128], rhs=PEc[:, l2:l2 + 1],
                                                          start=(l2 == 0), stop=(l2 == 15)),
                          reads=[BW1, BPE], writes=[Ppsb])
            sc.op("dve", lambda e: e.tensor_copy(out=b1sb, in_=psb[:, 0:2]), reads=[Ppsb], writes=[Bb1])
            for hc in range(2):
                b = nextbank()
                ps, Pps = self.bank(b), self.Pbank[b]
                for l2 in range(16):
                    sc.op("pe", lambda e, l2=l2: e.matmul(ps, lhsT=W1[:, l2, hc * 128:(hc + 1) * 128],
                                                          rhs=KC2[kv][:, 2 * l2:2 * l2 + 16 * 511 + 1:16], start=(l2 == 0), stop=(l2 == 15)),
                          reads=[BW1] + BKC[kv], writes=[Pps])
                sc.op("act", lambda e: e.activation(out=X, in_=ps, func=AF.Identity, bias=b1sb[:, hc:hc + 1], scale=1.0),
                      reads=[Pps, Bb1], writes=[BX])
                sc.op("dve", lambda e: e.tensor_tensor(out=X2, in0=X, in1=X, op=ALU.mult), reads=[BX], writes=[BX2])
                sc.op("dve", lambda e: e.tensor_scalar(out=X2, in0=X2, scalar1=0.044715, scalar2=1.0, op0=ALU.mult, op1=ALU.add),
                      reads=[BX2], writes=[BX2])
                sc.op("dve", lambda e: e.tensor_tensor(out=X2, in0=X2, in1=X, op=ALU.mult), reads=[BX2, BX], writes=[BX2])
                sc.op("act", lambda e: e.activation(out=TH, in_=X2, func=AF.Tanh, scale=0.7978845608028654), reads=[BX2], writes=[BTH])
                sc.op("dve", lambda e: e.scalar_tensor_tensor(out=TH, in0=TH, scalar=1.0, in1=X, op0=ALU.add, op1=ALU.mult),
                      reads=[BTH, BX], writes=[BTH])
                sc.op("dve", lambda e: e.tensor_scalar(out=Gb[hc][0], in0=TH, scalar1=0.5, scalar2=None, op0=ALU.mult),
                      reads=[BTH], writes=[Gb[hc][1]])
            if kv == 0:
                b = nextbank()
                ps, Pps = self.bank(b), self.Pbank[b]
                for hc in range(2):
                    sc.op("pe", lambda e, hc=hc: e.matmul(ps, lhsT=W2[:, hc, :], rhs=Gb[hc][0], start=(hc == 0), stop=(hc == 1)),
                          reads=[BW2, Gb[hc][1]], writes=[Pps])
                sc.op("act", lambda e: e.copy(out=KCMP[0:64, :], in_=ps[0:64, :]), reads=[Pps], writes=[BKCMP])
            else:
                for ct in range(4):
                    b = nextbank()
                    ps, Pps = self.bank(b), self.Pbank[b]
                    for hc in range(2):
                        sc.op("pe", lambda e, hc=hc: e.matmul(ps[:, 0:64], lhsT=Gb[hc][0][:, ct * 128:(ct + 1) * 128], rhs=W2[:, hc, 0:64],
                                                              start=(hc == 0), stop=(hc == 1)),
                              reads=[BW2, Gb[hc][1]], writes=[Pps])
                    sc.op("dve", lambda e: e.tensor_copy(out=VCMP[:, ct, 0:64], in_=ps[:, 0:64]), reads=[Pps], writes=[BVCMP])
        if "kcmp" in self.debug and bstop >= 3:
            dk = self.scratch("kcmp", [2, 64, 512], BF16)
            sc.dma("sp", dk[g], KCMP[0:64, :], reads=[BKCMP])
            dv = self.scratch("vcmp", [2, 128, 4 * 66], BF16)
            sc.dma("sp", dv[g], VCMP.rearrange("p a b -> p (a b)"), reads=[BVCMP])
        sc.barrier()
        self.pstack.close()
        self.pstack = contextlib.ExitStack()
        if bstop >= 4:
          _phaseB2(self, g, dict(QA4=QA4, BQA=BQA, KS=KS, BKS=BKS, KW=KW, BKW=BKW, VS=VS, BVS=BVS, VW=VW, BVW=BVW, SG=SG, BSG=BSG,
                                 KCMP=KCMP, BKCMP=BKCMP, VCMP=VCMP, BVCMP=BVCMP, rmat=rmat, cmask=cmask, causal4=causal4,
                                 wmask4=wmask4, ovx=ovx, ttab=ttab, pcadd=pcadd))
        sc.barrier()
        self.pstack.close()
        self.pstack = gstack
        self.end_phase()


Builder.phaseB = _phaseB


def _phaseB2(self, g, T):
    sc = self.sc
    QA4, BQA, KS, BKS, KW, BKW, VS, BVS, VW, BVW = (T[k] for k in ("QA4", "BQA", "KS", "BKS", "KW", "BKW", "VS", "BVS", "VW", "BVW"))
    SG, BSG, KCMP, BKCMP, VCMP, BVCMP = (T[k] for k in ("SG", "BSG", "KCMP", "BKCMP", "VCMP", "BVCMP"))
    RM, BRM = self.sb("RM", [128, 64, 128], BF16), Buf()
    sc.dma("pool", RM.rearrange("p a b -> p (a b)"), T["rmat"], writes=[BRM])
    CM, BCM = self.sb("CM", [128, 8, 512], BF16), Buf()
    sc.dma("pool", CM.rearrange("p a b -> p (a b)"), T["cmask"], writes=[BCM])
    CA4, BCA4 = self.sb("CA4", [128, 512], BF16), Buf()
    sc.dma("pool", CA4, T["causal4"], writes=[BCA4])
    WM4, BWM4 = self.sb("WM4", [128, 512], BF16), Buf()
    sc.dma("pool", WM4, T["wmask4"], writes=[BWM4])
    OVX, BOVX = self.sb("OVX", [128, 4, 130], BF16), Buf()
    sc.dma("pool", OVX.rearrange("p a b -> p (a b)"), T["ovx"], writes=[BOVX])
    TT, BTT = self.sb("TT", [128, 256], F32), Buf()
    sc.dma("sp", TT, T["ttab"], writes=[BTT])
    PC, BPC = self.sb("PC", [128, 128], F32), Buf()
    sc.dma("sp", PC, T["pcadd"], writes=[BPC])
    Er = [(self.sb("EB", [128, 512], BF16), Buf()) for _ in range(3)]
    E32 = [(self.sb("E16c", [128, 512], BF16), Buf()) for _ in range(4)]
    OB = [[(self.sb("OB", [128, 260], F32), Buf()) for _ in range(3)] for _ in range(2)]
    NST = [(self.sb("NST", [128, 512], BF16), Buf()) for _ in range(2)]
    dn, Bdn = self.sb("dn", [128, 4], F32), Buf()
    imp, Bimp = self.sb("imp", [128, 128], F32), Buf()
    rank, Brank = self.sb("rank", [128, 128], F32), Buf()
    r2, Br2 = self.sb("r2", [128, 128], F32), Buf()
    m8a, Bm8a = self.sb("m8a", [128, 8], F32), Buf()
    m8b, Bm8b = self.sb("m8b", [128, 8], F32), Buf()
    ns, Bns = self.sb("ns", [128, 128], BF16), Buf()
    dn3, Bdn3 = self.sb("dn3", [128, 12], F32), Buf()
    w3, Bw3 = self.sb("w3", [128, 12], F32), Buf()
    tmpo, Btmpo = self.sb("tmpo", [128, 64], F32), Buf()
    OBF, BOBF = self.sb("OBF", [128, 256], BF16), Buf()
    OBT = [(self.sb("OBT", [128, 2, 1024], BF16), Buf()) for _ in range(2)]
    QAf = QA4.rearrange("p n h q -> p n (h q)")
    if "b_nq" in self.debug:
        for o_, b_ in OBT:
            sc.op("pool", lambda e, o_=o_: e.memset(o_, 0.0), writes=[b_])
    def mk(n_):
        G = 2 * n_ + 1
        nct = (16 * n_ + 14) // 128 + 1
        c_ = [dict(br=0, n=n_, G=G, kt=ct, first=(ct == 0), last=(ct == nct - 1), nct=nct) for ct in range(nct)]
        wk = [kt for kt in range(G - 4, G + 1) if kt >= 0]
        w_ = [dict(br=2, n=n_, G=G, kt=kt, first=(kt == wk[0]), last=(kt == G)) for kt in wk]
        s_ = [dict(br=1, n=n_, G=G, kt=kt, first=(kt == 0), last=(kt == G)) for kt in range(G + 1)]
        return c_, w_, s_

    nq = self.debug.get("b_nq", 32)
    parts = [mk(n_) for n_ in range(nq)]
    steps = []
    for n_ in range(nq):
        steps += parts[n_][0]
        if n_ >= 1:
            steps += parts[n_ - 1][2]
        steps += parts[n_][1]
    if nq:
        steps += parts[nq - 1][2]
    rs = dict(s=0, e=0, o=0, nst=0)

    def stA(i):
        st = steps[i]
        br, n_, G, kt = st["br"], st["n"], st["G"], st["kt"]
        ss = rs["s"] % 2
        rs["s"] += 1
        st["ss"] = ss
        ps, Pps = self.bank(ss), self.Pbank[ss]
        q = QAf[:, n_, :]
        extras = []
        if br == 0:
            lhs, rl = KCMP[:, kt * 128:(kt + 1) * 128], [BKCMP]
            if st["last"]:
                extras.append((self.ident, CM[:, n_ % 8, :], [self.Bident, BCM]))
        else:
            Ksrc, Bk = (KS, BKS) if br == 1 else (KW, BKW)
            lhs, rl = Ksrc[:, kt * 128:(kt + 1) * 128], [Bk[kt // 4]]
            if br == 1:
                extras.append((RM[:, kt, :], NST[st_nst(n_)][0], [BRM, NST[st_nst(n_)][1]]))
            if kt == G:
                extras.append((self.ident, CA4, [self.Bident, BCA4]))
            if br == 2 and kt == G - 4:
                extras.append((self.ident, WM4, [self.Bident, BWM4]))
        sc.op("pe", lambda e: e.matmul(ps, lhsT=lhs, rhs=q, start=True, stop=(len(extras) == 0)), reads=rl + [BQA[n_ // 2]], writes=[Pps])
        for xi, (l, r, rd) in enumerate(extras):
            sc.op("pe", lambda e, l=l, r=r: e.matmul(ps, lhsT=l, rhs=r, start=False, stop=(xi == len(extras) - 1)), reads=rd, writes=[Pps])

    def st_nst(n_):
        return n_ % 2

    def stB(i):
        st = steps[i]
        br, kt = st["br"], st["kt"]
        ps, Pps = self.bank(st["ss"]), self.Pbank[st["ss"]]
        if br == 0:
            E, BE = E32[kt]
            st["Et"] = E32[kt]
        else:
            es = rs["e"] % 3
            rs["e"] += 1
            E, BE = Er[es]
            st["Et"] = Er[es]
        if kt == 0:
            col = 3 if br == 0 else 4
            bias = self.kvb[:, col:col + 1]
            sc.op("act", lambda e: e.activation(out=E, in_=ps, func=AF.Exp, scale=0.125, bias=bias), reads=[Pps, self.Bkvb], writes=[BE])
        else:
            sc.op("act", lambda e: e.activation(out=E, in_=ps, func=AF.Exp, scale=0.125), reads=[Pps], writes=[BE])

    def stC(i):
        st = steps[i]
        br, n_, G, kt = st["br"], st["n"], st["G"], st["kt"]
        E, BE = st["Et"]
        if st["first"]:
            rs["o"] += 1
            ob = 2 + rs["o"] % 2
            st["ob"] = ob
            sc.op("pe", lambda e: e.matmul(self.bank(ob)[:, 0:260], lhsT=self.ident, rhs=self.zeros[:, 0:260], start=True, stop=False),
                  reads=[self.Bident, self.Bzeros], writes=[self.Pbank[ob]])
        else:
            ob = steps[i - 1]["ob"]
            st["ob"] = ob
        pso, Po = self.bank(ob), self.Pbank[ob]
        if br == 0:
            V, Bv = VCMP[:, kt, 0:65], [BVCMP]
        elif br == 1:
            V, Bv = VS[:, kt, 0:65], [BVS[kt]]
        else:
            V, Bv = VW[:, kt, 0:65], [BVW[kt]]
        for h in range(4):
            sc.op("pe", lambda e, h=h: e.matmul(pso[:, h * 65:(h + 1) * 65], lhsT=E[:, h * 128:(h + 1) * 128], rhs=V,
                                                start=False, stop=(st["last"] and h == 3)),
                  reads=[BE] + Bv, writes=[Po])
        if not st["last"]:
            return
        par = n_ % 2
        O, BO = OB[par][br]
        sc.op("dve", lambda e: e.tensor_copy(out=O, in_=pso[:, 0:260]), reads=[Po], writes=[BO])
        if br == 0:
            nct = st["nct"]
            for h in range(4):
                bnk = 4 if h < 3 else 5
                off = (h % 3) * 129
                pi, Pi = self.bank(bnk), self.Pbank[bnk]
                for ct in range(nct):
                    sc.op("pe", lambda e, ct=ct: e.matmul(pi[:, off:off + 129], lhsT=E32[ct][0][:, h * 128:(h + 1) * 128], rhs=OVX[:, ct, 0:129],
                                                          start=(ct == 0), stop=(ct == nct - 1)),
                          reads=[E32[ct][1], BOVX], writes=[Pi])
            p4, p5 = self.bank(4), self.bank(5)
            sc.op("dve", lambda e: e.tensor_copy(out=dn[:, 0:3], in_=p4[:, 128:387:129]), reads=[self.Pbank[4]], writes=[Bdn])
            sc.op("dve", lambda e: e.tensor_copy(out=dn[:, 3:4], in_=p5[:, 128:129]), reads=[self.Pbank[5]], writes=[Bdn])
            sc.op("dve", lambda e: e.tensor_scalar(out=dn, in0=dn, scalar1=1e-30, scalar2=None, op0=ALU.max), reads=[Bdn], writes=[Bdn])
            sc.op("dve", lambda e: e.reciprocal(out=dn, in_=dn), reads=[Bdn], writes=[Bdn])
            sc.op("dve", lambda e: e.tensor_scalar(out=imp, in0=p4[:, 0:128], scalar1=dn[:, 0:1], scalar2=None, op0=ALU.mult),
                  reads=[self.Pbank[4], Bdn], writes=[Bimp])
            for h in range(1, 4):
                src = p4[:, h * 129:h * 129 + 128] if h < 3 else p5[:, 0:128]
                sc.op("dve", lambda e, h=h, src=src: e.scalar_tensor_tensor(out=imp, in0=src, scalar=dn[:, h:h + 1], in1=imp,
                                                                           op0=ALU.mult, op1=ALU.add),
                      reads=[self.Pbank[4 if h < 3 else 5], Bdn, Bimp], writes=[Bimp])
            sc.op("dve", lambda e: e.tensor_tensor(out=rank, in0=imp, in1=TT[:, 126 - 2 * G:126 - 2 * G + 128], op=ALU.add),
                  reads=[Bimp, BTT], writes=[Brank])
            sc.op("dve", lambda e: e.tensor_tensor(out=rank, in0=rank, in1=PC, op=ALU.add), reads=[Brank, BPC], writes=[Brank])
            sc.op("dve", lambda e: e.max(out=m8a, in_=rank), reads=[Brank], writes=[Bm8a])
            sc.op("dve", lambda e: e.match_replace(out=r2, in_to_replace=m8a, in_values=rank, imm_value=-3.0e38),
                  reads=[Brank, Bm8a], writes=[Br2])
            sc.op("dve", lambda e: e.max(out=m8b, in_=r2), reads=[Br2], writes=[Bm8b])
            sc.op("dve", lambda e: e.tensor_scalar(out=ns, in0=rank, scalar1=m8b[:, 7:8], scalar2=1.0, op0=ALU.is_ge, op1=ALU.subtract),
                  reads=[Brank, Bm8b], writes=[Bns])
            pT = self.bank(6).bitcast(BF16)[:, 0:128]
            sc.op("pe", lambda e: e.transpose(pT, ns, self.ident), reads=[Bns, self.Bident], writes=[self.Pbank[6]])
            N_, BN_ = NST[st_nst(n_)]
            for h in range(4):
                sc.op("dve", lambda e, h=h: e.tensor_copy(out=N_[:, h * 128:(h + 1) * 128], in_=pT), reads=[self.Pbank[6]], writes=[BN_])
            if "dbg_sel" in self.debug:
                dsel = self.dbg_sel
                sc.dma("sp", dsel[g, n_], N_[:, 0:128], reads=[BN_])
        if br == 1:
            Os = OB[par]
            for b_ in range(3):
                sc.op("dve", lambda e, b_=b_: e.tensor_copy(out=dn3[:, b_ * 4:(b_ + 1) * 4], in_=Os[b_][0][:, 64:260:65]),
                      reads=[Os[b_][1]], writes=[Bdn3])
            sc.op("dve", lambda e: e.tensor_scalar(out=dn3, in0=dn3, scalar1=1e-30, scalar2=None, op0=ALU.max), reads=[Bdn3], writes=[Bdn3])
            sc.op("dve", lambda e: e.reciprocal(out=dn3, in_=dn3), reads=[Bdn3], writes=[Bdn3])
            sgv = SG[:, n_, :].rearrange("p (h b) -> p b h", b=3)
            sc.op("dve", lambda e: e.tensor_tensor(out=w3.rearrange("p (b h) -> p b h", b=3), in0=dn3.rearrange("p (b h) -> p b h", b=3),
                                                   in1=sgv, op=ALU.mult),
                  reads=[Bdn3, BSG], writes=[Bw3])
            for h in range(4):
                sl = slice(h * 65, h * 65 + 64)
                sc.op("dve", lambda e: e.tensor_scalar(out=tmpo, in0=Os[0][0][:, sl], scalar1=w3[:, h:h + 1], scalar2=None, op0=ALU.mult),
                      reads=[Os[0][1], Bw3], writes=[Btmpo])
                sc.op("dve", lambda e: e.scalar_tensor_tensor(out=tmpo, in0=Os[1][0][:, sl], scalar=w3[:, 4 + h:5 + h], in1=tmpo,
                                                              op0=ALU.mult, op1=ALU.add),
                      reads=[Os[1][1], Bw3, Btmpo], writes=[Btmpo])
                sc.op("dve", lambda e: e.scalar_tensor_tensor(out=OBF[:, h * 64:(h + 1) * 64], in0=Os[2][0][:, sl], scalar=w3[:, 8 + h:9 + h],
                                                              in1=tmpo, op0=ALU.mult, op1=ALU.add),
                      reads=[Os[2][1], Bw3, Btmpo], writes=[BOBF])
            stg, Bstg = OBT[(n_ // 8) % 2]
            pT7 = self.bank(7).bitcast(BF16)
            for ch in range(2):
                sc.op("pe", lambda e, ch=ch: e.transpose(pT7[:, ch * 128:(ch + 1) * 128], OBF[:, ch * 128:(ch + 1) * 128], self.ident),
                      reads=[BOBF, self.Bident], writes=[self.Pbank[7]])
            sc.op("dve", lambda e: e.tensor_copy(out=stg[:, :, (n_ % 8) * 128:(n_ % 8 + 1) * 128],
                                                 in_=pT7[:, 0:256].rearrange("p (c q) -> p c q", c=2)),
                  reads=[self.Pbank[7]], writes=[Bstg])
            if n_ % 8 == 7 or i == len(steps) - 1:
                n0 = (n_ // 8) * 8
                for ch in range(2):
                    r0 = g * 256 + ch * 128
                    sc.dma("sp", self.obT[r0:r0 + 128, n0 * 128:n0 * 128 + 1024], stg[:, ch, :], reads=[Bstg], writes=[self.BobT])

    pipeline(len(steps), [stA, stB, stC])


def _tail_chunks():
    ch = []
    for co in range(8):
        ch += [("ga", co, 8), ("gb", co, 8), ("ua", co, 4), ("ub", co, 4)]
    for co in range(8):
        ch.append(("out", co, 8))
    for hc in range(32):
        ch.append(("w1", hc, 8))
    for co in range(8):
        ch.append(("w2", co, 32))
    for co in range(8):
        ch += [("pg", co, 8), ("ple", co, 2)]
    offs, o = [], 0
    for (_, _, kc) in ch:
        offs.append(o)
        o += kc * 16384
    return ch, offs, o


TAIL_CH, TAIL_OFF, TAIL_N = _tail_chunks()


def _cast_tail_weights(self):
    sc = self.sc
    wt = self.inp("wtail", [TAIL_N // 1024, 1024])
    self.wtb = self.scratch("wtb", [TAIL_N // 1024, 1024], BF16)
    rows = TAIL_N // 1024
    nseg = 8
    self.Bwtb = Buf()
    per = rows // nseg
    self._cast_segs = []
    for i in range(nseg):
        r0, r1 = i * per, (rows if i == nseg - 1 else (i + 1) * per)
        self._cast_segs.append((self.wtb[r0:r1, :], wt[r0:r1, :]))


def _cast_some(self, n):
    for _ in range(n):
        if self._cast_segs:
            o, i_ = self._cast_segs.pop(0)
            self.sc.dma("pool", o, i_, writes=[self.Bwtb])


Builder.cast_some = _cast_some


Builder.cast_tail_weights = _cast_tail_weights


def _phaseT(self, tiles=(0, 1, 2, 3)):
    sc = self.sc
    self.begin_phase()
    pT = self.inp("pT", [256, SO])
    outT = self.nc.dram_tensor("outT", [D, SO], F32, kind="ExternalOutput").ap()
    outr = outT.rearrange("(c p) t -> p c t", p=128)
    wflat = self.wtb.rearrange("r c -> (r c)")
    H, BH = self.sb("tH", [128, 8, 1024], F32), Buf()
    NB, BNB = self.sb("tNB", [128, 8, 1024], BF16), Buf()
    MIXf = self.sb("tMIX", [128, 8192], BF16)
    MIX, BMIX = MIXf.rearrange("p (c t) -> p c t", c=8), Buf()
    SQ = MIXf.bitcast(F32).rearrange("p (c t) -> p c t", c=8)
    HIDf = self.sb("tHID", [128, 32 * 1024], BF16)
    HID, BHID = HIDf.rearrange("p (c t) -> p c t", c=32), Buf()
    OAB, BOAB = HID[:, 0:8, :], Buf()
    NBX, BNBX = HID[:, 8:16, :], Buf()
    OUTB = HIDf.bitcast(F32)[:, 8192:16384].rearrange("p (c t) -> p c t", c=8)
    SGA = [(self.sb("tSG", [128, 1024], F32), Buf()) for _ in range(2)]
    TMP = [(self.sb("tTMP", [128, 1024], F32), Buf()) for _ in range(2)]
    RL = [(self.sb("tRL", [128, 512], BF16), Buf()) for _ in range(2)]
    PTt, BPT = self.sb("tPT", [128, 2, 1024], BF16), Buf()
    WR = [(self.sb("tW", [128, 32 * 128], BF16), Buf()) for _ in range(3)]
    rs, Brs = self.sb("trs", [128, 512], F32), Buf()
    ri, Bri = self.sb("tri", [128, 512], F32), Buf()
    ACC, BACC = self.sb("tACC", [128, 1024], F32), [Buf(), Buf()]
    SQC, BSQC = self.sb("tSQC", [128, 1024], F32), [Buf(), Buf()]
    hsl = lambda hh: slice(hh * 512, (hh + 1) * 512)

    def stat_update(co, hh, Bg):
        if co == 0:
            sc.op("act", lambda e: e.activation(out=ACC[:, hsl(hh)], in_=H[:, co, hsl(hh)], func=AF.Square),
                  reads=[Bg], writes=[BACC[hh]])
        else:
            sc.op("act", lambda e: e.activation(out=SQC[:, hsl(hh)], in_=H[:, co, hsl(hh)], func=AF.Square),
                  reads=[Bg], writes=[BSQC[hh]])
            sc.op("pool", lambda e: e.tensor_tensor(out=ACC[:, hsl(hh)], in0=ACC[:, hsl(hh)], in1=SQC[:, hsl(hh)], op=ALU.add),
                  reads=[BSQC[hh]], writes=[BACC[hh]])

    def stat_update_pool(co, Bg):
        if co == 0:
            sc.op("pool", lambda e: e.tensor_tensor(out=ACC, in0=H[:, co, :], in1=H[:, co, :], op=ALU.mult),
                  reads=[Bg], writes=BACC)
        else:
            sc.op("pool", lambda e: e.tensor_tensor(out=SQC, in0=H[:, co, :], in1=H[:, co, :], op=ALU.mult),
                  reads=[Bg], writes=BSQC)
            sc.op("pool", lambda e: e.tensor_tensor(out=ACC, in0=ACC, in1=SQC, op=ALU.add), reads=BSQC, writes=BACC)

    def norm_finish(out_ap, Bout, gidx):
        for hh in range(2):
            ps, Pps = self.bank(6 + hh), self.Pbank[6 + hh]
            sc.op("pe", lambda e: e.matmul(ps, lhsT=self.ones32, rhs=ACC[:, hsl(hh)], start=True, stop=True),
                  reads=[self.Bones32, BACC[hh]], writes=[Pps])
            sc.op("act", lambda e: e.activation(out=rs, in_=ps, func=AF.Sqrt, scale=1.0 / D, bias=self.epsc[:, 0:1]),
                  reads=[Pps, self.Beps], writes=[Brs])
            sc.op("dve", lambda e: e.reciprocal(out=ri, in_=rs), reads=[Brs], writes=[Bri])
            for c in range(8):
                sc.op("dve", lambda e, c=c: e.scalar_tensor_tensor(
                    out=out_ap[:, c, hsl(hh)], in0=H[:, c, hsl(hh)], scalar=self.gains[:, gidx, c:c + 1],
                    in1=ri, op0=ALU.mult, op1=ALU.mult),
                    reads=[BH, self.Bgains, Bri], writes=[Bout])
    nch = len(TAIL_CH)
    state = dict(issued=0, bank=0, sg=0, tmp=0, rl=0)
    total = nch * len(tiles)

    def issue(gi):
        ci = gi % nch
        kc = TAIL_CH[ci][2]
        w, bw = WR[gi % 3]
        src = wflat[TAIL_OFF[ci]:TAIL_OFF[ci] + kc * 16384].rearrange("(p x) -> p x", p=128)
        sc.dma("sp", w[:, 0:kc * 128], src, reads=[self.Bwtb], writes=[bw])

    def wchunk(gi):
        while state["issued"] < min(gi + 3, total):
            issue(state["issued"])
            state["issued"] += 1
        w, bw = WR[gi % 3]
        kc = TAIL_CH[gi % nch][2]
        return w[:, 0:kc * 128].rearrange("p (k j) -> p k j", j=128), bw

    def nbank():
        state["bank"] = (state["bank"] + 1) % 6
        return state["bank"]

    def mm_group(w, bw, kc, rhs_fn, rbufs, hh):
        b = nbank()
        ps, Pps = self.bank(b), self.Pbank[b]
        for k in range(kc):
            sc.op("pe", lambda e, k=k: e.matmul(ps, lhsT=w[:, k, :], rhs=rhs_fn(k, hh), start=(k == 0), stop=(k == kc - 1)),
                  reads=[bw] + rbufs, writes=[Pps])
        return ps, Pps

    def load_nb_oab(tt):
        c0 = tt * 1024
        nsrc = self.nr[:, :, 2048 * tt:2048 * tt + 2048].rearrange("p c (b x) -> p c b x", b=8)[:, :, :, 128:256]
        for c_ in range(8):
            sc.dma("sp", NBX[:, c_, :].rearrange("p (b x) -> p b x", b=8), nsrc[:, c_], reads=self.BnT, writes=[BNBX])

    def load_oab(tt):
        c0 = tt * 1024
        sc.dma("sp", OAB[:, 0:4, :], self.oaT.rearrange("(c p) t -> p c t", p=128)[:, :, c0:c0 + 1024], reads=[self.BoaT], writes=[BOAB])
        sc.dma("sp", OAB[:, 4:8, :], self.obT.rearrange("(c p) t -> p c t", p=128)[:, :, c0:c0 + 1024], reads=[self.BobT], writes=[BOAB])

    def load_h(tt):
        c0 = tt * 1024
        xsrc = self.xr[:, :, 2048 * tt:2048 * tt + 2048].rearrange("p c (b x) -> p c b x", b=8)[:, :, :, 128:256]
        for c_ in range(8):
            sc.dma("sp", H[:, c_, :].rearrange("p (b x) -> p b x", b=8), xsrc[:, c_], writes=[BH])
        sc.dma("pool", PTt, pT.rearrange("(c p) t -> p c t", p=128)[:, :, c0:c0 + 1024], writes=[BPT])

    gi = 0
    for ti, tt in enumerate(tiles):
        nxt = tiles[ti + 1] if ti + 1 < len(tiles) else None
        c0 = tt * 1024
        hs = lambda hh: slice(hh * 512, (hh + 1) * 512)
        if ti == 0:
            load_oab(tt)
            load_nb_oab(tt)
            load_h(tt)
        for co in range(8):
            wga, bga = wchunk(gi); gi += 1
            sga, Bsga = SGA[0]
            for hh in range(2):
                ps, Pps = mm_group(wga, bga, 8, lambda k, hh: NBX[:, k, hs(hh)], [BNBX], hh)
                sc.op("act", lambda e: e.activation(out=sga[:, hs(hh)], in_=ps, func=AF.Sigmoid), reads=[Pps], writes=[Bsga])
            wgb, bgb = wchunk(gi); gi += 1
            sgb, Bsgb = SGA[1]
            for hh in range(2):
                ps, Pps = mm_group(wgb, bgb, 8, lambda k, hh: NBX[:, k, hs(hh)], [BNBX], hh)
                sc.op("act", lambda e: e.activation(out=sgb[:, hs(hh)], in_=ps, func=AF.Sigmoid), reads=[Pps], writes=[Bsgb])
            wua, bua = wchunk(gi); gi += 1
            ta, Bta = TMP[0]
            for hh in range(2):
                ps, Pps = mm_group(wua, bua, 4, lambda k, hh: OAB[:, k, hs(hh)], [BOAB], hh)
                sc.op("dve", lambda e: e.tensor_tensor(out=ta[:, hs(hh)], in0=ps, in1=sga[:, hs(hh)], op=ALU.mult),
                      reads=[Pps, Bsga], writes=[Bta])
            wub, bub = wchunk(gi); gi += 1
            tb_, Btb = TMP[1]
            for hh in range(2):
                ps, Pps = mm_group(wub, bub, 4, lambda k, hh: OAB[:, 4 + k, hs(hh)], [BOAB], hh)
                sc.op("dve", lambda e: e.tensor_tensor(out=tb_[:, hs(hh)], in0=ps, in1=sgb[:, hs(hh)], op=ALU.mult),
                      reads=[Pps, Bsgb], writes=[Btb])
            sc.op("pool", lambda e: e.tensor_tensor(out=MIX[:, co, :], in0=ta, in1=tb_, op=ALU.add), reads=[Bta, Btb], writes=[BMIX])
        for co in range(8):
            w, bw = wchunk(gi); gi += 1
            for hh in range(2):
                ps, Pps = mm_group(w, bw, 8, lambda k, hh: MIX[:, k, hs(hh)], [BMIX], hh)
                Bg = Buf()
                sc.op("dve", lambda e: e.tensor_tensor(out=H[:, co, hs(hh)], in0=H[:, co, hs(hh)], in1=ps, op=ALU.add),
                      reads=[Pps, BH], writes=[BH, Bg])
                stat_update(co, hh, Bg)
        norm_finish(NB, BNB, 1)
        for hc in range(32):
            w, bw = wchunk(gi); gi += 1
            for hh in range(2):
                ps, Pps = mm_group(w, bw, 8, lambda k, hh: NB[:, k, hs(hh)], [BNB], hh)
                r_, Br = RL[state["rl"] % 2]
                state["rl"] += 1
                sc.op("act", lambda e: e.activation(out=r_, in_=ps, func=AF.Relu), reads=[Pps], writes=[Br])
                sc.op("dve", lambda e: e.scalar_tensor_tensor(out=HID[:, hc, hs(hh)], in0=ps, scalar=0.0, in1=r_, op0=ALU.max, op1=ALU.mult),
                      reads=[Pps, Br], writes=[BOAB if hc < 8 else (BNBX if hc < 16 else BHID)])
        for co in range(8):
            w, bw = wchunk(gi); gi += 1
            for hh in range(2):
                ps, Pps = mm_group(w, bw, 32, lambda k, hh: HID[:, k, hs(hh)], [BHID, BOAB, BNBX], hh)
                Bg = Buf()
                sc.op("dve", lambda e: e.tensor_tensor(out=H[:, co, hs(hh)], in0=H[:, co, hs(hh)], in1=ps, op=ALU.add),
                      reads=[Pps, BH], writes=[BH, Bg])
                stat_update(co, hh, Bg)
        if nxt is not None:
            load_oab(nxt)
            load_nb_oab(nxt)
        norm_finish(NB, BNB, 2)
        for co in range(8):
            wpg, bpg = wchunk(gi); gi += 1
            sga, Bsga = SGA[co % 2]
            for hh in range(2):
                ps, Pps = mm_group(wpg, bpg, 8, lambda k, hh: NB[:, k, hs(hh)], [BNB], hh)
                sc.op("act", lambda e: e.activation(out=sga[:, hs(hh)], in_=ps, func=AF.Sigmoid), reads=[Pps], writes=[Bsga])
            wpl, bpl = wchunk(gi); gi += 1
            ta, Bta = TMP[co % 2]
            for hh in range(2):
                ps, Pps = mm_group(wpl, bpl, 2, lambda k, hh: PTt[:, k, hs(hh)], [BPT], hh)
                sc.op("dve", lambda e: e.tensor_tensor(out=ta[:, hs(hh)], in0=ps, in1=sga[:, hs(hh)], op=ALU.mult),
                      reads=[Pps, Bsga], writes=[Bta])
            Bg = Buf()
            sc.op("dve", lambda e: e.tensor_tensor(out=H[:, co, :], in0=H[:, co, :], in1=ta, op=ALU.add), reads=[Bta, BH], writes=[BH, Bg])
            stat_update_pool(co, Bg)
        norm_finish(OUTB, BHID, 3)
        if nxt is not None:
            load_h(nxt)
        sc.dma("sp", outr[:, :, c0:c0 + 1024], OUTB, reads=[BHID], is_output=True)
    self.end_phase()


Builder.phaseT = _phaseT


def _own_tokens(p):
    gb = np.arange(32) * 2 + (1 if p == 1 else 0)
    return (gb[:, None] * 128 + np.arange(128)[None, :]).reshape(-1)


_PROGRAM = {}


def kernel(**inputs):
    inputs = {k: np.asarray(v) for k, v in inputs.items()}
    x = np.asarray(inputs["x"], np.float32)
    pin = np.asarray(inputs["p"], np.float32)[0]
    w = _prep_weights(inputs)
    c = _consts_common()
    if "B" not in _PROGRAM:
        import os
        B = build_program(upto=os.environ.get("KERNEL_UPTO", "all"))
        B.sc.finish()
        _PROGRAM["B"] = B
    B = _PROGRAM["B"]
    maps = []
    for core in range(8):
        b, p = core // 2, core % 2
        m = dict(w)
        m.update(c)
        m["kvb"] = _kvb(p)
        m["pcadd"] = _pcadd(p)
        m["xT"] = _virtual_xT(x[b], p)
        m["pT"] = np.ascontiguousarray(pin[b][_own_tokens(p)].T)
        maps.append({k: np.ascontiguousarray(v, dtype=np.float32) for k, v in m.items() if k in B.din})
    res = run_bass_kernel_spmd(B.nc, maps, core_ids=list(range(8)))
    out = np.zeros((4, S, D), np.float32)
    for core in range(8):
        b, p = core // 2, core % 2
        if "outT" in res.results[core]:
            out[b, _own_tokens(p)] = np.asarray(res.results[core]["outT"], np.float32).T
    return out
```

```python
import contextlib
import numpy as np
import concourse.bass as bass
import concourse.mybir as mybir
from concourse.bass_utils import run_bass_kernel_spmd

F32 = mybir.dt.float32
BF16 = mybir.dt.bfloat16
AF = mybir.ActivationFunctionType
ALU = mybir.AluOpType

S = 8192
SO = 4096
D = 1024
NEGM = -30000.0
SLOPES = [2.0 ** (-(h + 1)) for h in range(8)]
EPS = 1e-6
SEM_LIMIT = 30000
NDMA = 20


class Tok:
    __slots__ = ("es", "sem", "val")

    def __init__(self, es=None, sem=None, val=None):
        self.es, self.sem, self.val = es, sem, val


class Buf:
    __slots__ = ("w", "r", "name")

    def __init__(self, name=""):
        self.w = None
        self.r = {}
        self.name = name


class EngState:
    def __init__(self, name, eng):
        self.name, self.eng = name, eng
        self.sem = None
        self.cnt = 0
        self.last = None
        self.pending = []
        self.waited = {}


class _PEProxy:
    def __init__(self, eng):
        self.eng = eng
        self.stop = False

    def matmul(self, *a, **k):
        self.stop = bool(k.get("stop"))
        return self.eng.matmul(*a, **k)

    def transpose(self, *a, **k):
        self.stop = True
        return self.eng.transpose(*a, **k)


class Sched:
    def __init__(self, nc):
        self.nc = nc
        self.E = {
            "pe": EngState("pe", nc.tensor),
            "act": EngState("act", nc.scalar),
            "dve": EngState("dve", nc.vector),
            "pool": EngState("pool", nc.gpsimd),
            "sp": EngState("sp", nc.sync),
        }
        self.nsem = 0
        self.dma_sems = [self._sem() for _ in range(NDMA)]
        self.dma_cnt = [0] * NDMA
        self.dma_last = [None] * NDMA
        self.dma_rr = {"sp": 0, "pool": 0, "act": 0}
        self.uid = 0
        self.out_toks = []

    def _sem(self):
        self.nsem += 1
        return self.nc.alloc_semaphore("s%d" % self.nsem)

    def resolve(self, tok):
        if tok.val is not None:
            return
        es = tok.es
        if es.sem is None or es.cnt >= SEM_LIMIT:
            es.sem = self._sem()
            es.cnt = 0
        es.cnt += 1
        es.last.then_inc(es.sem, 1)
        for t in es.pending:
            t.sem, t.val = es.sem, es.cnt
        es.pending = []

    def wait(self, es, tok):
        if tok is None:
            return
        self.resolve(tok)
        k = id(tok.sem)
        if es.waited.get(k, 0) >= tok.val:
            return
        es.eng.wait_ge(tok.sem, tok.val)
        es.waited[k] = tok.val

    def _deps(self, es, reads, writes, is_dma):
        pe = (not is_dma) and es.name == "pe"
        for b in reads:
            t = b.w
            if t is None or (pe and t.es is es):
                continue
            self.wait(es, t)
        for b in writes:
            t = b.w
            transitive = t is not None and t.es is es and any(r.es is not es for r in b.r.values())
            if t is not None and not (pe and t.es is es) and not transitive:
                self.wait(es, t)
            for k, t in b.r.items():
                if not (pe and t.es is es):
                    self.wait(es, t)

    def op(self, q, fn, reads=(), writes=(), mark=None):
        es = self.E[q]
        self._deps(es, reads, writes, False)
        if q == "pe":
            px = _PEProxy(es.eng)
            ins = fn(px)
            eager = px.stop if mark is None else mark
        else:
            ins = fn(es.eng)
            eager = True if mark is None else mark
        es.last = ins
        tok = Tok(es=es)
        es.pending.append(tok)
        es.last_tok = tok
        if eager:
            self.resolve(tok)
        for b in reads:
            b.r[q] = tok
        for b in writes:
            b.w = tok
            b.r = {}
        return tok

    def dma(self, q, out, in_, reads=(), writes=(), is_output=False, **kw):
        es = self.E[q]
        self._deps(es, reads, writes, True)
        half = NDMA // 2
        k = self.dma_rr[q]
        self.dma_rr[q] = (k + 1) % half
        i = k + (half if q == "pool" else 0)
        if self.dma_last[i] is not None:
            self.wait(es, self.dma_last[i])
        self.dma_cnt[i] += 16
        if q == "pool":
            kw.setdefault("max_dma_last_dim", 4096)
        ins = es.eng.dma_start(out=out, in_=in_, **kw)
        ins.then_inc(self.dma_sems[i], 16)
        tok = Tok(es=None, sem=self.dma_sems[i], val=self.dma_cnt[i])
        self.dma_last[i] = tok
        self.uid += 1
        for b in reads:
            b.r["dma%d" % self.uid] = tok
        for b in writes:
            b.w = tok
            b.r = {}
        if is_output:
            self.out_toks.append(tok)
        return tok

    def barrier(self):
        toks = []
        for es in self.E.values():
            if getattr(es, "last_tok", None) is not None:
                toks.append(es.last_tok)
        for t in toks:
            self.resolve(t)
        alltoks = toks + [t for t in self.dma_last if t is not None]
        for es in self.E.values():
            for t in alltoks:
                if t.es is es:
                    continue
                self.wait(es, t)

    def fence(self, q):
        es = self.E[q]
        if getattr(es, "last_tok", None) is not None:
            self.wait(es, es.last_tok)

    def finish(self):
        es = self.E["sp"]
        for t in self.dma_last:
            if t is not None:
                self.wait(es, t)


class Ring:
    def __init__(self, items):
        self.items = items
        self.i = -1

    def next(self):
        self.i = (self.i + 1) % len(self.items)
        return self.items[self.i]


def pipeline(n, stages):
    ns = len(stages)
    for t in range(n + ns - 1):
        for s in range(ns):
            i = t - s
            if 0 <= i < n:
                stages[s](i)


class Builder:
    def __init__(self, debug=None):
        self.debug = debug or {}
        nc = bass.Bass("TRN2", target_bir_lowering=False)
        self.nc = nc
        self.sc = Sched(nc)
        self.din = {}
        self.psum_banks = [nc.alloc_psum_tensor("bank%d" % i, [128, 512], F32) for i in range(8)]
        self.gstack = contextlib.ExitStack()
        self.Pbank = [Buf() for _ in range(8)]
        self.PbankD = [Buf() for _ in range(2)]
        self.pstack = None
        self.nalloc = 0

    def inp(self, name, shape, dt=F32):
        t = self.nc.dram_tensor(name, list(shape), dt, kind="ExternalInput").ap()
        self.din[name] = t
        return t

    def scratch(self, name, shape, dt):
        kind = "ExternalOutput" if name in self.debug else "Internal"
        return self.nc.dram_tensor(name, list(shape), dt, kind=kind).ap()

    def sb(self, name, shape, dt, persistent=False):
        st = self.gstack if (persistent or self.pstack is None) else self.pstack
        self.nalloc += 1
        t = st.enter_context(self.nc.sbuf_tensor("%s_%d" % (name, self.nalloc), list(shape), dt))
        return t.ap()

    def begin_phase(self):
        self.pstack = contextlib.ExitStack()

    def end_phase(self):
        self.sc.barrier()
        self.pstack.close()
        self.pstack = None

    def bank(self, i, dt=F32):
        t = self.psum_banks[i]
        if dt is F32:
            return t.ap()
        return t.bitcast(dt).ap() if hasattr(t, "bitcast") else t.ap()


    def setup_consts(self):
        sc = self.sc
        self.ident = self.sb("ident", [128, 128], BF16, True)
        self.Bident = Buf()
        sc.dma("pool", self.ident, self.inp("ident", [128, 128]), writes=[self.Bident])
        self.ones32 = self.sb("ones32", [128, 128], F32, True)
        self.Bones32 = Buf()
        sc.op("pool", lambda e: e.memset(self.ones32, 1.0), writes=[self.Bones32])
        self.onesh = self.sb("onesh", [128, 2, 128], BF16, True)
        self.Bonesh = Buf()
        sc.op("pool", lambda e: e.memset(self.onesh, 0.0), writes=[self.Bonesh])
        sc.op("pool", lambda e: e.memset(self.onesh[:, 0, 0:64], 1.0), writes=[self.Bonesh])
        sc.op("pool", lambda e: e.memset(self.onesh[:, 1, 64:128], 1.0), writes=[self.Bonesh])
        self.zeros = self.sb("zeros", [128, 512], BF16, True)
        self.Bzeros = Buf()
        sc.op("pool", lambda e: e.memset(self.zeros, 0.0), writes=[self.Bzeros])
        self.kvb = self.sb("kvb", [128, 8], F32, True)
        self.Bkvb = Buf()
        sc.dma("sp", self.kvb, self.inp("kvb", [128, 8]), writes=[self.Bkvb])
        self.epsc = self.sb("epsc", [128, 1], F32, True)
        self.Beps = Buf()
        sc.op("pool", lambda e: e.memset(self.epsc, EPS), writes=[self.Beps])
        self.gains = self.sb("gains", [128, 4, 8], F32, True)
        self.Bgains = Buf()
        sc.dma("sp", self.gains, self.inp("gains", [128, 4, 8]), writes=[self.Bgains])

    def rmsnorm_tile(self, x_ap, Bx, out_ap, Bout, gi, T, tmp, ps_bank):
        sc = self.sc
        sq, Bsq, rs, Brs, ri, Bri = tmp
        for h0 in range(0, T, 512):
            ps = self.bank(ps_bank)
            Pps = self.Pbank[ps_bank]
            sc.op("act", lambda e: e.activation(out=sq[:, :, 0:512], in_=x_ap[:, :, h0:h0 + 512], func=AF.Square),
                  reads=[Bx], writes=[Bsq])
            for n_ in (4, 2, 1):
                sc.op("dve", lambda e, n_=n_: e.tensor_tensor(out=sq[:, 0:n_, 0:512], in0=sq[:, 0:n_, 0:512],
                                                              in1=sq[:, n_:2 * n_, 0:512], op=ALU.add),
                      reads=[Bsq], writes=[Bsq])
            sc.op("pe", lambda e: e.matmul(ps, lhsT=self.ones32, rhs=sq[:, 0, 0:512], start=True, stop=True),
                  reads=[self.Bones32, Bsq], writes=[Pps])
            sc.op("act", lambda e: e.activation(out=rs, in_=ps, func=AF.Sqrt, scale=1.0 / D, bias=self.epsc[:, 0:1]),
                  reads=[Pps, self.Beps], writes=[Brs])
            sc.op("dve", lambda e: e.reciprocal(out=ri, in_=rs), reads=[Brs], writes=[Bri])
            for c in range(8):
                sc.op("dve", lambda e, c=c: e.scalar_tensor_tensor(
                    out=out_ap[:, c, h0:h0 + 512], in0=x_ap[:, c, h0:h0 + 512], scalar=self.gains[:, gi, c:c + 1],
                    in1=ri, op0=ALU.mult, op1=ALU.mult),
                    reads=[Bx, self.Bgains, Bri], writes=[Bout])

    def norm_tmp(self):
        return (self.sb("nsq", [128, 8, 512], F32), Buf(), self.sb("nrs", [128, 512], F32), Buf(),
                self.sb("nri", [128, 512], F32), Buf())

    def phase0(self):
        sc = self.sc
        self.begin_phase()
        xT = self.inp("xT", [D, S])
        self.nT = self.scratch("nT", [D, S], BF16)
        self.BnT = [Buf() for _ in range(16)]
        xr = xT.rearrange("(c p) t -> p c t", p=128)
        self.xr = xr
        nr = self.nT.rearrange("(c p) t -> p c t", p=128)
        self.nr = nr
        xt = [(self.sb("p0x", [128, 8, 512], F32), Buf()) for i in range(3)]
        nt = [(self.sb("p0n", [128, 8, 512], BF16), Buf()) for i in range(2)]
        tmps = [self.norm_tmp() for _ in range(2)]

        def load(j):
            if j < 16:
                a, b = xt[j % 3]
                sc.dma("sp", a, xr[:, :, j * 512:(j + 1) * 512], writes=[b])

        def stA(j):
            a, b = xt[j % 3]
            sq, Bsq, rs, Brs, ri, Bri = tmps[j % 2]
            ps, Pps = self.bank(6 + j % 2), self.Pbank[6 + j % 2]
            sc.op("act", lambda e: e.activation(out=sq, in_=a, func=AF.Square), reads=[b], writes=[Bsq])
            for n_ in (4, 2, 1):
                sc.op("dve", lambda e, n_=n_: e.tensor_tensor(out=sq[:, 0:n_, :], in0=sq[:, 0:n_, :], in1=sq[:, n_:2 * n_, :], op=ALU.add),
                      reads=[Bsq], writes=[Bsq])
            sc.op("pe", lambda e: e.matmul(ps, lhsT=self.ones32, rhs=sq[:, 0, :], start=True, stop=True),
                  reads=[self.Bones32, Bsq], writes=[Pps])

        def stB(j):
            a, b = xt[j % 3]
            na, nb = nt[j % 2]
            sq, Bsq, rs, Brs, ri, Bri = tmps[j % 2]
            ps, Pps = self.bank(6 + j % 2), self.Pbank[6 + j % 2]
            sc.op("act", lambda e: e.activation(out=rs, in_=ps, func=AF.Sqrt, scale=1.0 / D, bias=self.epsc[:, 0:1]),
                  reads=[Pps, self.Beps], writes=[Brs])
            sc.op("dve", lambda e: e.reciprocal(out=ri, in_=rs), reads=[Brs], writes=[Bri])
            for c in range(8):
                sc.op("dve", lambda e, c=c: e.scalar_tensor_tensor(
                    out=na[:, c, :], in0=a[:, c, :], scalar=self.gains[:, 0, c:c + 1], in1=ri, op0=ALU.mult, op1=ALU.mult),
                    reads=[b, self.Bgains, Bri], writes=[nb])
            sc.dma("pool", nr[:, :, j * 512:(j + 1) * 512], na, reads=[nb], writes=[self.BnT[j]])
            load(j + 3)

        load(0)
        load(1)
        load(2)
        pipeline(16, [stA, stB])
        self.end_phase()

    A_CFG = [(1, 128, 0), (4, 64, 512), (16, 64, 768)]

    def q_own_ap(self, t, dil, r, mt):
        if dil == 1:
            ob = (mt - 1) // 2
            return t[:, ob * 128:(ob + 1) * 128]
        if dil == 4:
            return t[:, 256 * mt:256 * mt + 256].rearrange("p (a x) -> p a x", a=2)[:, :, r::4]
        return t[:, 1024 * mt:1024 * mt + 1024].rearrange("p (a x) -> p a x", a=8)[:, :, r::16]

    @staticmethod
    def v3(ap, dil):
        if dil == 1:
            return ap
        return ap.rearrange("p (a x) -> p a x", a=(2 if dil == 4 else 8))

    def phaseA(self, hps=(0, 1, 2, 3)):
        sc = self.sc
        self.begin_phase()
        wA = self.inp("wA", [4, 128, 8, 384])
        abias = self.inp("abias", [4, 128, 1024])
        self.oaT = self.scratch("oaT", [512, SO], BF16)
        self.BoaT = Buf()
        WA, BWA = self.sb("WA", [128, 8, 384], BF16), Buf()
        AB, BAB = self.sb("AB", [128, 1024], BF16), Buf()
        KT = self.sb("KT", [128, S], BF16)
        VT = self.sb("VT", [128, S], BF16)
        BKT = [Buf() for _ in range(16)]
        BVT = [Buf() for _ in range(16)]
        QTZ = self.sb("QTZ", [128, 2, SO], BF16)
        QT = [QTZ[:, 0, :], QTZ[:, 1, :]]
        BQT = [[Buf() for _ in range(16)] for _ in range(2)]
        onesf, Bonesf = self.sb("onesf", [128, 128], BF16), Buf()
        sc.op("pool", lambda e: e.memset(onesf, 1.0), writes=[Bonesf])
        accn, accd = self.sb("accn", [128, SO], F32), self.sb("accd", [128, SO], F32)
        Baccn, Baccd = Buf(), Buf()
        ntr = [(self.sb("ntA", [128, 8, 512], BF16), Buf()) for _ in range(2)]
        Vt = [[(self.sb("Vt", [128, 128], BF16), Buf()) for _ in range(4)] for _ in range(2)]
        Er = [(self.sb("EA", [128, 256], BF16), Buf()) for _ in range(3)]
        ostg = [(self.sb("oastg", [128, 1024], BF16), Buf()) for _ in range(2)]
        sc.op("pool", lambda e: e.memset(QT[0][64:128, :], 0.0), writes=[b for b in BQT[0]])
        sc.op("pool", lambda e: e.memset(QT[1][0:64, :], 0.0), writes=[b for b in BQT[1]])
        for i in range(4):
            sc.op("pool", lambda e, i=i: e.memset(Vt[0][i][0][:, 64:128], 0.0), writes=[Vt[0][i][1]])
            sc.op("pool", lambda e, i=i: e.memset(Vt[1][i][0][:, 0:64], 0.0), writes=[Vt[1][i][1]])
        TB = self.debug.get("a_tb", [6, 7, 6, 7])
        psT = [self.bank(b).bitcast(BF16)[:, 0:128] for b in TB]
        if self.debug.get("a_init"):
            sc.op("dve", lambda e: e.memset(accn, 0.0), writes=[Baccn])
            sc.op("dve", lambda e: e.memset(accd, 1.0), writes=[Baccd])
        PT = [self.Pbank[b] for b in TB]
        for hp in hps:
            sc.dma("pool", WA, wA[hp], writes=[BWA])
            sc.dma("pool", AB, abias[hp], writes=[BAB])
            self.cast_some(2)
            pi = 0
            for j in range(16):
                na, nb = ntr[j % 2]
                sc.dma("sp", na, self.nr[:, :, j * 512:(j + 1) * 512], reads=[self.BnT[j]], writes=[nb])
                for which in range(3):
                    bi = pi % 2
                    pi += 1
                    ps, Pps = self.bank(bi), self.Pbank[bi]
                    if which == 0:
                        rhs = lambda c: na[:, c, :].rearrange("p (a x) -> p a x", a=2)[:, :, 128:256]
                        outp = ps[:, 0:256].rearrange("p (a x) -> p a x", a=2)
                    else:
                        rhs = lambda c: na[:, c, :]
                        outp = ps
                    for c in range(8):
                        sc.op("pe", lambda e, c=c: e.matmul(outp, lhsT=WA[:, c, which * 128:(which + 1) * 128], rhs=rhs(c),
                                                            start=(c == 0), stop=(c == 7)),
                              reads=[BWA, nb], writes=[Pps])
                    if which == 0:
                        sc.op("act", lambda e: e.copy(out=QT[0][0:64, j * 256:(j + 1) * 256], in_=ps[0:64, 0:256]),
                              reads=[Pps], writes=[BQT[0][j]])
                        sc.op("dve", lambda e: e.tensor_copy(out=QT[1][64:128, j * 256:(j + 1) * 256], in_=ps[64:128, 0:256]),
                              reads=[Pps], writes=[BQT[1][j]])
                    elif which == 1:
                        sc.op("act", lambda e: e.copy(out=KT[:, j * 512:(j + 1) * 512], in_=ps), reads=[Pps], writes=[BKT[j]])
                    else:
                        sc.op("dve", lambda e: e.tensor_copy(out=VT[:, j * 512:(j + 1) * 512], in_=ps), reads=[Pps], writes=[BVT[j]])
            steps = []
            for ci, (dil, Nq, boff) in enumerate(self.A_CFG):
                nmt = 64 // dil
                for r in range(dil):
                    for mt in range(nmt):
                        has_q = (mt % 2 == 1) if dil == 1 else True
                        kts = ([mt - 1] if mt >= 1 else []) + [mt]
                        if not has_q:
                            steps.append(dict(tr_only=True, ci=ci, dil=dil, r=r, kt=mt))
                            continue
                        for k_i, kt in enumerate(kts):
                            steps.append(dict(tr_only=False, ci=ci, dil=dil, Nq=Nq, boff=boff, r=r, mt=mt, kt=kt,
                                              kind=(0 if kt == mt else 1), first=(k_i == 0), last=(k_i == len(kts) - 1),
                                              do_tr=(kt == mt)))
            ring_state = dict(s=0, e=0, n=0)
            for st_ in steps:
                st_["sslot"] = None

            def tile_bufs(dil, r, kt):
                lo = 128 * dil * kt + r
                hi = lo + dil * 127
                return lo, list(range(lo // 512, hi // 512 + 1))

            def stA(i):
                st = steps[i]
                dil, r, kt = st["dil"], st["r"], st["kt"]
                lo, tb = tile_bufs(dil, r, kt)
                if (st["tr_only"] or st["do_tr"]) and not (self.debug.get("a_notr2") and i == 2):
                    slot = kt % 4
                    src = VT[:, lo:lo + dil * 127 + 1:dil]
                    sc.op("pe", lambda e: e.transpose(psT[slot], src, self.ident),
                          reads=[BVT[x] for x in tb] + [self.Bident], writes=[PT[slot]])
                    sc.op("dve", lambda e: e.tensor_copy(out=Vt[0][slot][0][:, 0:64], in_=psT[slot][:, 0:64]),
                          reads=[PT[slot]], writes=[Vt[0][slot][1]])
                    sc.op(self.debug.get("a_ev2", "dve"), lambda e: (e.copy if self.debug.get("a_ev2", "dve") == "act" else e.tensor_copy)(out=Vt[1][slot][0][:, 64:128], in_=psT[slot][:, 64:128]),
                          reads=[PT[slot]], writes=[Vt[1][slot][1]])
                if st["tr_only"] or (self.debug.get("a_nos2") and i == 2):
                    return
                Nq, boff, mt, kind = st["Nq"], st["boff"], st["mt"], st["kind"]
                ss = ring_state["s"] % 2
                ring_state["s"] += 1
                st["ss"] = ss
                ps_s = self.bank(2 + ss)[:, 0:2 * Nq]
                Pss = self.Pbank[2 + ss]
                bcol = boff + kind * 2 * Nq
                sc.op("pe", lambda e: e.matmul(ps_s, lhsT=self.ident, rhs=AB[:, bcol:bcol + 2 * Nq], start=True, stop=False),
                      reads=[self.Bident, BAB], writes=[Pss])
                kcols = KT[:, lo:lo + dil * 127 + 1:dil]
                if dil == 1:
                    ob_ = (mt - 1) // 2
                    qap = QTZ[:, :, ob_ * 128:(ob_ + 1) * 128]
                    oap = ps_s.rearrange("p (h x) -> p h x", h=2)
                    qb = [BQT[0][(mt - 1) // 4], BQT[1][(mt - 1) // 4]]
                elif dil == 4:
                    qap = QTZ[:, :, 256 * mt:256 * mt + 256].rearrange("p h (a x) -> p h a x", a=2)[:, :, :, r::4]
                    oap = ps_s.rearrange("p (h a x) -> p h a x", h=2, a=2)
                    qb = [BQT[0][mt], BQT[1][mt]]
                else:
                    qap = QTZ[:, :, 1024 * mt:1024 * mt + 1024].rearrange("p h (a x) -> p h a x", a=8)[:, :, :, r::16]
                    oap = ps_s.rearrange("p (h a x) -> p h a x", h=2, a=8)
                    qb = [BQT[h][4 * mt + x] for h in range(2) for x in range(4)]
                sc.op("pe", lambda e: e.matmul(oap, lhsT=kcols, rhs=qap, start=False, stop=True),
                      reads=[BKT[x] for x in tb] + qb, writes=[Pss])

            def stB(i):
                st = steps[i]
                if st["tr_only"]:
                    return
                Nq = st["Nq"]
                ps_s = self.bank(2 + st["ss"])[:, 0:2 * Nq]
                es = ring_state["e"] % 3
                ring_state["e"] += 1
                st["es"] = es
                E, BE = Er[es]
                if st["kt"] == 0:
                    ci = st["ci"]
                    sc.op("act", lambda e: e.activation(out=E[:, 0:2 * Nq], in_=ps_s, func=AF.Exp, scale=0.125,
                                                        bias=self.kvb[:, ci:ci + 1]),
                          reads=[self.Pbank[2 + st["ss"]], self.Bkvb], writes=[BE])
                else:
                    sc.op("act", lambda e: e.activation(out=E[:, 0:2 * Nq], in_=ps_s, func=AF.Exp, scale=0.125),
                          reads=[self.Pbank[2 + st["ss"]]], writes=[BE])

            def stC(i):
                st = steps[i]
                if st["tr_only"]:
                    return
                Nq, dil, r, mt = st["Nq"], st["dil"], st["r"], st["mt"]
                E, BE = Er[st["es"]]
                if st["first"]:
                    ring_state["n"] += 1
                nsl = ring_state["n"] % 2
                slot = st["kt"] % 4
                ps_n, Pn = self.bank(4 + nsl)[:, 0:Nq], self.Pbank[4 + nsl]
                ps_d, Pd = self.bank(nsl)[:, 0:2 * Nq], self.Pbank[nsl]
                for h in range(2):
                    sc.op("pe", lambda e, h=h: e.matmul(ps_n, lhsT=Vt[h][slot][0], rhs=E[:, h * Nq:(h + 1) * Nq],
                                                        start=(st["first"] and h == 0), stop=(st["last"] and h == 1)),
                          reads=[Vt[h][slot][1], BE], writes=[Pn])
                sc.op("pe", lambda e: e.matmul(ps_d, lhsT=onesf, rhs=E[:, 0:2 * Nq], start=st["first"], stop=st["last"]),
                      reads=[Bonesf, BE], writes=[Pd])
                if st["last"]:
                    an = self.q_own_ap(accn, dil, r, mt)
                    ad0 = self.q_own_ap(accd[0:64, :], dil, r, mt)
                    ad1 = self.q_own_ap(accd[64:128, :], dil, r, mt)
                    pd0, pd1 = self.v3(ps_d[0:64, 0:Nq], dil), self.v3(ps_d[64:128, Nq:2 * Nq], dil)
                    if st["ci"] == 0:
                        sc.op("dve", lambda e: e.tensor_copy(out=an, in_=self.v3(ps_n, dil)), reads=[Pn], writes=[Baccn])
                        sc.op("dve", lambda e: e.tensor_copy(out=ad0, in_=pd0), reads=[Pd], writes=[Baccd])
                        sc.op("dve", lambda e: e.tensor_copy(out=ad1, in_=pd1), reads=[Pd], writes=[Baccd])
                    else:
                        sc.op("dve", lambda e: e.tensor_tensor(out=an, in0=an, in1=self.v3(ps_n, dil), op=ALU.add), reads=[Pn, Baccn], writes=[Baccn])
                        sc.op("dve", lambda e: e.tensor_tensor(out=ad0, in0=ad0, in1=pd0, op=ALU.add), reads=[Pd, Baccd], writes=[Baccd])
                        sc.op("dve", lambda e: e.tensor_tensor(out=ad1, in0=ad1, in1=pd1, op=ALU.add), reads=[Pd, Baccd], writes=[Baccd])

            if "a_nsteps" in self.debug:
                steps = steps[:self.debug["a_nsteps"]]
            if self.debug.get("a_stages", 3) == 3:
                pipeline(len(steps), [stA, stB, stC])
            elif self.debug.get("a_stages") == 2:
                pipeline(len(steps), [stA, stB])
            else:
                pipeline(len(steps), [stA])
            Bfin = Buf()
            sc.fence("dve")
            for q4 in range(4):
                sl = slice(q4 * 1024, (q4 + 1) * 1024)
                sc.op("dve", lambda e: e.reciprocal(out=accd[:, sl], in_=accd[:, sl]), reads=[Baccd], writes=[Baccd])
                og, Bog = ostg[q4 % 2]
                sc.op("dve", lambda e: e.tensor_tensor(out=og, in0=accn[:, sl], in1=accd[:, sl], op=ALU.mult),
                      reads=[Baccd, Baccn], writes=[Bog])
                sc.dma("sp", self.oaT[hp * 128:(hp + 1) * 128, sl], og, reads=[Bog], writes=[self.BoaT])
        self.end_phase()


def _a_bias_tables():
    out = np.zeros((4, 128, 1024), np.float32)
    j = np.arange(128)[:, None]
    for hp in range(4):
        for (dil, Nq, boff) in Builder.A_CFG:
            if dil == 1:
                u = np.arange(128)
            elif dil == 4:
                u = (32 + 64 * np.arange(2)[:, None] + np.arange(32)[None, :]).reshape(-1)
            else:
                u = (8 + 16 * np.arange(8)[:, None] + np.arange(8)[None, :]).reshape(-1)
            u = u[None, :]
            for kind in range(2):
                dist = (u - j) if kind == 0 else (128 + u - j)
                valid = (dist >= 0) & (dist <= 128)
                for h in range(2):
                    sl = SLOPES[2 * hp + h]
                    val = np.where(valid, -8.0 * sl * dil * dist, NEGM).astype(np.float32)
                    c0 = boff + kind * 2 * Nq + h * Nq
                    out[hp, :, c0:c0 + Nq] = val
    return out


def _consts_common():
    c = {}
    c["ident"] = np.eye(128, dtype=np.float32)
    c["abias"] = _a_bias_tables()
    t = np.arange(S)
    c["kaug"] = np.stack([np.ones(S), np.ones(S), 128.0 * (t // 128), (t % 128).astype(np.float64)], 0).astype(np.float32)
    pos = 16 * np.arange(512) + 31
    c["caug"] = np.stack([np.ones(512), np.ones(512), 128.0 * (pos // 128), (pos % 128).astype(np.float64)], 0).astype(np.float32)
    qaug = np.zeros((2, 4, 32, 4, 128), np.float32)
    for g in range(2):
        for h in range(4):
            s = SLOPES[4 * g + h]
            qaug[g, 0, :, h, :] = (-8.0 * s * 128.0 * (2 * np.arange(32) + 1))[:, None]
            qaug[g, 1, :, h, :] = (-8.0 * s * np.arange(128))[None, :]
            qaug[g, 2, :, h, :] = 8.0 * s
            qaug[g, 3, :, h, :] = 8.0 * s
    c["qaug"] = qaug.reshape(2, 4, 32 * 512)
    j = np.arange(128)[:, None, None]
    kt = np.arange(64)[None, :, None]
    k = np.arange(128)[None, None, :]
    c["rmat"] = np.where(j == 2 * kt + k // 64, 30000.0, 0.0).astype(np.float32).reshape(128, 64 * 128)
    jj = np.arange(128)[:, None, None]
    v = np.arange(8)[None, :, None]
    u = (np.arange(512) % 128)[None, None, :]
    c["cmask"] = np.where(16 * jj <= 256 * v + 97 + u, 0.0, NEGM).astype(np.float32).reshape(128, 8 * 512)
    j2 = np.arange(128)[:, None]
    u2 = (np.arange(512) % 128)[None, :]
    c["causal4"] = np.where(j2 <= u2, 0.0, NEGM).astype(np.float32)
    c["wmask4"] = np.where(j2 > u2, 0.0, NEGM).astype(np.float32)
    cc = np.arange(512)[:, None]
    jb = np.arange(128)[None, :]
    ov = np.maximum(np.minimum(cc + 2, 4 * (jb + 1)) - np.maximum(cc, 4 * jb), 0).astype(np.float32)
    ovx = np.zeros((512, 130), np.float32)
    ovx[:, 0:128] = ov
    ovx[:, 128] = 1.0
    c["ovx"] = np.ascontiguousarray(ovx.reshape(4, 128, 130).transpose(1, 0, 2)).reshape(128, 4 * 130)
    i = np.arange(128)[:, None]
    rel = np.arange(256)[None, :] - 126
    cur = (i >= 64).astype(np.int64)
    tt = np.where(rel > cur, -1e30, np.where((rel == cur) | (rel == cur - 1), 1e9, 0.0))
    c["ttab"] = tt.astype(np.float32)
    return c


def _pcadd(p):
    row = np.zeros(128, np.float32)
    if p == 1:
        row[0] = 1e9
    else:
        row[0:2] = -1e30
        row[2] = 1e9
    return np.tile(row[None, :], (128, 1))


def _kvb(p):
    kvb = np.zeros((128, 8), np.float32)
    if p == 0:
        kvb[:, 0] = NEGM
        kvb[0:32, 1] = NEGM
        kvb[0:8, 2] = NEGM
        kvb[0:8, 3] = NEGM
        kvb[:, 4] = NEGM
    return kvb


def _prep_weights(inputs):
    w = {}
    w_in = np.asarray(inputs["w_in"][0], np.float32)
    qa, ka, va = w_in[:, 0:512], w_in[:, 512:1024], w_in[:, 1024:1536]
    wA = np.zeros((4, 128, 8, 384), np.float32)
    for hp in range(4):
        cols = np.concatenate([qa[:, hp * 128:(hp + 1) * 128], ka[:, hp * 128:(hp + 1) * 128], va[:, hp * 128:(hp + 1) * 128]], 1)
        wA[hp] = cols.reshape(8, 128, 384).transpose(1, 0, 2)
    w["wA"] = wA
    qb = w_in[:, 1536:2048]
    wB = np.zeros((2, 128, 8, NCB), np.float32)
    for g in range(2):
        cols = np.zeros((D, NCB), np.float32)
        for h in range(4):
            cols[:, h * 128:h * 128 + 64] = qb[:, (4 * g + h) * 64:(4 * g + h + 1) * 64]
        kc = w_in[:, 2048 + 64 * g:2048 + 64 * g + 64]
        vc = w_in[:, 2176 + 64 * g:2176 + 64 * g + 64]
        cols[:, 512:576] = w_in[:, 2304 + 64 * g:2304 + 64 * g + 64]
        cols[:, 640:704] = w_in[:, 2560 + 64 * g:2560 + 64 * g + 64]
        cols[:, 768:832] = kc
        cols[:, 832:896] = kc
        cols[:, 896:960] = vc
        cols[:, 960:1024] = vc
        cols[:, 1024:1088] = w_in[:, 2432 + 64 * g:2432 + 64 * g + 64]
        cols[:, 1088:1152] = w_in[:, 2688 + 64 * g:2688 + 64 * g + 64]
        cols[:, 1152:1164] = w_in[:, 2816 + 12 * g:2816 + 12 * g + 12]
        wB[g] = cols.reshape(8, 128, NCB).transpose(1, 0, 2)
    w["wB"] = wB
    w["wcmp1"] = np.stack([np.asarray(inputs[k][0], np.float32).reshape(16, 128, 256).transpose(1, 0, 2) for k in ("w_ck1", "w_cv1")], 0)
    w["pecmp"] = np.stack([np.asarray(inputs[k][0], np.float32).reshape(16, 128).T for k in ("pe_ck", "pe_cv")], 0)
    w2 = np.zeros((2, 128, 2, 128), np.float32)
    for i, k in enumerate(("w_ck2", "w_cv2")):
        w2[i, :, :, 0:64] = np.asarray(inputs[k][0], np.float32).reshape(2, 128, 64).transpose(1, 0, 2)
    w["wcmp2"] = w2
    def chunk(W, co):
        K_ = W.shape[0]
        return np.ascontiguousarray(W[:, co * 128:(co + 1) * 128].reshape(K_ // 128, 128, 128).transpose(1, 0, 2)).reshape(-1)
    srcs = {"ga": w_in[:, 2840:3864], "gb": w_in[:, 3864:4888], "ua": np.asarray(inputs["w_up_a"][0], np.float32),
            "ub": np.asarray(inputs["w_up_b"][0], np.float32), "out": np.asarray(inputs["w_out"][0], np.float32),
            "w1": np.asarray(inputs["w_mlp1"][0], np.float32), "w2": np.asarray(inputs["w_mlp2"][0], np.float32),
            "pg": np.asarray(inputs["w_ple_gate"][0], np.float32), "ple": np.asarray(inputs["w_ple"][0], np.float32)}
    w["wtail"] = np.concatenate([chunk(srcs[nm], co) for (nm, co, kc) in TAIL_CH]).reshape(TAIL_N // 1024, 1024)
    gains = np.stack([inputs["norm_mix_g"][0], inputs["norm_mlp_g"][0], inputs["norm_ple_g"][0], inputs["norm_final_g"]], 0)
    w["gains"] = np.ascontiguousarray(np.asarray(gains, np.float32).reshape(4, 8, 128).transpose(2, 0, 1))
    return w


def _virtual_xT(xb, p):
    xt = np.ascontiguousarray(xb.T)
    if p == 1:
        return xt
    out = np.zeros_like(xt)
    out[:, 128:] = xt[:, :S - 128]
    return out


def build_program(debug=None, upto="all"):
    B = Builder(debug=debug)
    B.setup_consts()
    B.cast_tail_weights()
    B.phase0()
    if upto == "p0":
        return B
    if not (debug or {}).get("skipA"):
        B.phaseA(hps=(debug or {}).get("hps", (0, 1, 2, 3)))
    elif (debug or {}).get("ext_o"):
        B.oaT, B.BoaT = B.inp("oaT", [512, SO], BF16), Buf()
        B.obT, B.BobT = B.inp("obT", [512, SO], BF16), Buf()
    if upto == "A":
        return B
    if not (debug or {}).get("skipB"):
        B.phaseB(groups=(debug or {}).get("groups", (0, 1)))
    if upto == "B":
        return B
    B.cast_some(99)
    B.phaseT(tiles=(debug or {}).get("tiles", (0, 1, 2, 3)))
    return B


NCB = 1164


def _phaseB(self, groups=(0, 1)):
    sc = self.sc
    wB = self.inp("wB", [2, 128, 8, NCB])
    wcmp1 = self.inp("wcmp1", [2, 128, 16, 256])
    pecmp = self.inp("pecmp", [2, 128, 16])
    wcmp2 = self.inp("wcmp2", [2, 128, 2, 128])
    kaug = self.inp("kaug", [4, S])
    caug = self.inp("caug", [4, 512])
    qaug = self.inp("qaug", [2, 4, 32 * 512])
    rmat = self.inp("rmat", [128, 64 * 128])
    cmask = self.inp("cmask", [128, 8 * 512])
    causal4 = self.inp("causal4", [128, 512])
    wmask4 = self.inp("wmask4", [128, 512])
    ovx = self.inp("ovx", [128, 4 * 130])
    ttab = self.inp("ttab", [128, 256])
    pcadd = self.inp("pcadd", [128, 128])
    self.obT = self.scratch("obT", [512, SO], BF16)
    self.BobT = Buf()
    for g in groups:
        self.begin_phase()
        gstack = self.pstack
        QA = self.sb("QA", [68, 32 * 512], BF16)
        BQA = [Buf() for _ in range(16)]
        KS, KW = self.sb("KS", [68, S], BF16), self.sb("KW", [68, S], BF16)
        BKS, BKW = [Buf() for _ in range(16)], [Buf() for _ in range(16)]
        VS, VW = self.sb("VS", [128, 64, 66], BF16), self.sb("VW", [128, 64, 66], BF16)
        BVS, BVW = [Buf() for _ in range(64)], [Buf() for _ in range(64)]
        SG, BSG = self.sb("SG", [128, 32, 12], F32), Buf()
        KCMP, BKCMP = self.sb("KCMP", [68, 512], BF16), Buf()
        VCMP, BVCMP = self.sb("VCMP", [128, 4, 66], BF16), Buf()
        QA4 = QA.rearrange("p (n h q) -> p n h q", n=32, h=4)
        sc.dma("pool", KS[64:68, :], kaug, writes=BKS)
        sc.dma("pool", KW[64:68, :], kaug, writes=BKW)
        sc.dma("pool", QA[64:68, :], qaug[g], writes=BQA)
        sc.dma("pool", KCMP[64:68, :], caug, writes=[BKCMP])
        sc.op("pool", lambda e: e.memset(VS[:, :, 64:65], 1.0), writes=BVS)
        sc.op("pool", lambda e: e.memset(VW[:, :, 64:65], 1.0), writes=BVW)
        sc.op("pool", lambda e: e.memset(VCMP[:, :, 64:65], 1.0), writes=[BVCMP])
        self.pstack = contextlib.ExitStack()
        WB, BWB = self.sb("WB", [128, 8, NCB], BF16), Buf()
        sc.dma("pool", WB, wB[g], writes=[BWB])
        ntr = [(self.sb("ntB", [128, 8, 512], BF16), Buf()) for _ in range(2)]
        KC2 = [self.sb("KC2", [128, S + 32], BF16) for _ in range(2)]
        BKC = [[Buf() for _ in range(16)] for _ in range(2)]
        for kv in range(2):
            sc.op("pool", lambda e, kv=kv: e.memset(KC2[kv][:, S - 16:S + 32], 0.0), writes=[BKC[kv][15]])
        bk = [0]

        def nextbank():
            bk[0] = (bk[0] + 1) % 6
            return bk[0]

        ev = [0]

        def evac(out, in_, reads, writes, func=None):
            ev[0] += 1
            if func is not None:
                sc.op("act", lambda e: e.activation(out=out, in_=in_, func=func), reads=reads, writes=writes)
            elif ev[0] % 2 == 0:
                sc.op("act", lambda e: e.copy(out=out, in_=in_), reads=reads, writes=writes)
            else:
                sc.op("dve", lambda e: e.tensor_copy(out=out, in_=in_), reads=reads, writes=writes)

        bstop = self.debug.get("b_stop", 99)
        for j in range(16 if bstop >= 2 else 0):
            na, nb = ntr[j % 2]
            sc.dma("sp", na, self.nr[:, :, j * 512:(j + 1) * 512], reads=[self.BnT[j]], writes=[nb])
            own = lambda c: na[:, c, :].rearrange("p (a x) -> p a x", a=2)[:, :, 128:256]
            bskip = self.debug.get("b_skip", "")
            for h in range(0 if "q" in bskip else 4):
                b = nextbank()
                ps, Pps = self.bank(b), self.Pbank[b]
                for c in range(8):
                    sc.op("pe", lambda e, c=c: e.matmul(ps[:, 0:256].rearrange("p (a x) -> p a x", a=2),
                                                        lhsT=WB[:, c, h * 128:(h + 1) * 128], rhs=own(c), start=(c == 0), stop=(c == 7)),
                          reads=[BWB, nb], writes=[Pps])
                evac(QA4[0:64, 2 * j:2 * j + 2, h, :], ps[0:64, 0:256].rearrange("p (a x) -> p a x", a=2), [Pps], [BQA[j]])
            for which, (dst, Bd) in enumerate(((KS, BKS), (KW, BKW)) if "k" not in bskip else ()):
                b = nextbank()
                ps, Pps = self.bank(b), self.Pbank[b]
                c0 = 512 + which * 128
                for c in range(8):
                    sc.op("pe", lambda e, c=c: e.matmul(ps, lhsT=WB[:, c, c0:c0 + 128], rhs=na[:, c, :], start=(c == 0), stop=(c == 7)),
                          reads=[BWB, nb], writes=[Pps])
                evac(dst[0:64, j * 512:(j + 1) * 512], ps[0:64, :], [Pps], [Bd[j]])
            for kv in range(0 if "c" in bskip else 2):
                b = nextbank()
                ps, Pps = self.bank(b), self.Pbank[b]
                c0 = 768 + kv * 128
                for c in range(8):
                    sc.op("pe", lambda e, c=c: e.matmul(ps, lhsT=WB[:, c, c0:c0 + 128], rhs=na[:, c, :], start=(c == 0), stop=(c == 7)),
                          reads=[BWB, nb], writes=[Pps])
                evac(KC2[kv][0:64, j * 512:(j + 1) * 512], ps[0:64, :], [Pps], [BKC[kv][j]])
                if j == 0:
                    evac(KC2[kv][64:128, 0:511], ps[64:128, 1:512], [Pps], [BKC[kv][j]])
                else:
                    evac(KC2[kv][64:128, j * 512 - 1:j * 512 + 511], ps[64:128, :], [Pps], [BKC[kv][j], BKC[kv][j - 1]])
            for tb in range(0 if "t" in bskip else 4):
                b = nextbank()
                ps, Pps = self.bank(b), self.Pbank[b]
                for c in range(8):
                    sc.op("pe", lambda e, c=c: e.matmul(ps[:, 0:140], lhsT=na[:, c, tb * 128:(tb + 1) * 128], rhs=WB[:, c, 1024:1164],
                                                        start=(c == 0), stop=(c == 7)),
                          reads=[BWB, nb], writes=[Pps])
                kt = 4 * j + tb
                sc.op("dve", lambda e: e.tensor_copy(out=VS[:, kt, 0:64], in_=ps[:, 0:64]), reads=[Pps], writes=[BVS[kt]])
                sc.op("dve", lambda e: e.tensor_copy(out=VW[:, kt, 0:64], in_=ps[:, 64:128]), reads=[Pps], writes=[BVW[kt]])
                if tb % 2 == 1:
                    sc.op("dve", lambda e: e.tensor_copy(out=SG[:, 2 * j + tb // 2, :], in_=ps[:, 128:140]), reads=[Pps], writes=[BSG])
        if bstop >= 2:
            sc.op("act", lambda e: e.activation(out=SG, in_=SG, func=AF.Sigmoid), reads=[BSG], writes=[BSG])
        W1, BW1 = self.sb("W1c", [128, 16, 256], BF16), Buf()
        PEc, BPE = self.sb("PEc", [128, 16], BF16), Buf()
        W2, BW2 = self.sb("W2c", [128, 2, 128], BF16), Buf()
        b1sb, Bb1 = self.sb("b1sb", [128, 2], F32), Buf()
        X, BX = self.sb("cX", [128, 512], F32), Buf()
        X2, BX2 = self.sb("cX2", [128, 512], F32), Buf()
        TH, BTH = self.sb("cTH", [128, 512], F32), Buf()
        Gb = [(self.sb("cG", [128, 512], BF16), Buf()) for _ in range(2)]
        for kv in range(2 if bstop >= 3 else 0):
            sc.dma("pool", W1, wcmp1[kv], writes=[BW1])
            sc.dma("pool", PEc, pecmp[kv], writes=[BPE])
            sc.dma("pool", W2, wcmp2[kv], writes=[BW2])
            b = nextbank()
            psb, Ppsb = self.bank(b), self.Pbank[b]
            for hc in range(2):
                for l2 in range(16):
                    sc.op("pe", lambda e, l2=l2: e.matmul(psb[:, hc:hc + 1], lhsT=W1[:, l2, hc * 128:(hc + 1) * 128], rhs=PEc[:, l2:l2 + 1],
                                                          start=(l2 == 0), stop=(l2 == 15)),
                          reads=[BW1, BPE], writes=[Ppsb])
            sc.op("dve", lambda e: e.tensor_copy(out=b1sb, in_=psb[:, 0:2]), reads=[Ppsb], writes=[Bb1])
            for hc in range(2):
                b = nextbank()
                ps, Pps = self.bank(b), self.Pbank[b]
                for l2 in range(16):
                    sc.op("pe", lambda e, l2=l2: e.matmul(ps, lhsT=W1[:, l2, hc * 128:(hc + 1) * 128],
                                                          rhs=KC2[kv][:, 2 * l2:2 * l2 + 16 * 511 + 1:16], start=(l2 == 0), stop=(l2 == 15)),
                          reads=[BW1] + BKC[kv], writes=[Pps])
                sc.op("act", lambda e: e.activation(out=X, in_=ps, func=AF.Identity, bias=b1sb[:, hc:hc + 1], scale=1.0),
                      reads=[Pps, Bb1], writes=[BX])
                sc.op("dve", lambda e: e.tensor_tensor(out=X2, in0=X, in1=X, op=ALU.mult), reads=[BX], writes=[BX2])
                sc.op("dve", lambda e: e.tensor_scalar(out=X2, in0=X2, scalar1=0.044715, scalar2=1.0, op0=ALU.mult, op1=ALU.add),
                      reads=[BX2], writes=[BX2])
                sc.op("dve", lambda e: e.tensor_tensor(out=X2, in0=X2, in1=X, op=ALU.mult), reads=[BX2, BX], writes=[BX2])
                sc.op("act", lambda e: e.activation(out=TH, in_=X2, func=AF.Tanh, scale=0.7978845608028654), reads=[BX2], writes=[BTH])
                sc.op("dve", lambda e: e.scalar_tensor_tensor(out=TH, in0=TH, scalar=1.0, in1=X, op0=ALU.add, op1=ALU.mult),
                      reads=[BTH, BX], writes=[BTH])
                sc.op("dve", lambda e: e.tensor_scalar(out=Gb[hc][0], in0=TH, scalar1=0.5, scalar2=None, op0=ALU.mult),
                      reads=[BTH], writes=[Gb[hc][1]])
            if kv == 0:
                b = nextbank()
                ps, Pps = self.bank(b), self.Pbank[b]
                for hc in range(2):
                    sc.op("pe", lambda e, hc=hc: e.matmul(ps, lhsT=W2[:, hc, :], rhs=Gb[hc][0], start=(hc == 0), stop=(hc == 1)),
                          reads=[BW2, Gb[hc][1]], writes=[Pps])
                sc.op("act", lambda e: e.copy(out=KCMP[0:64, :], in_=ps[0:64, :]), reads=[Pps], writes=[BKCMP])
            else:
                for ct in range(4):
                    b = nextbank()
                    ps, Pps = self.bank(b), self.Pbank[b]
                    for hc in range(2):
                        sc.op("pe", lambda e, hc=hc: e.matmul(ps[:, 0:64], lhsT=Gb[hc][0][:, ct * 128:(ct + 1) * 128], rhs=W2[:, hc, 0:64],
                                                              start=(hc == 0), stop=(hc == 1)),
                              reads=[BW2, Gb[hc][1]], writes=[Pps])
                    sc.op("dve", lambda e: e.tensor_copy(out=VCMP[:, ct, 0:64], in_=ps[:, 0:64]), reads=[Pps], writes=[BVCMP])
        if "kcmp" in self.debug and bstop >= 3:
            dk = self.scratch("kcmp", [2, 64, 512], BF16)
            sc.dma("sp", dk[g], KCMP[0:64, :], reads=[BKCMP])
            dv = self.scratch("vcmp", [2, 128, 4 * 66], BF16)
            sc.dma("sp", dv[g], VCMP.rearrange("p a b -> p (a b)"), reads=[BVCMP])
        sc.barrier()
        self.pstack.close()
        self.pstack = contextlib.ExitStack()
        if bstop >= 4:
          _phaseB2(self, g, dict(QA4=QA4, BQA=BQA, KS=KS, BKS=BKS, KW=KW, BKW=BKW, VS=VS, BVS=BVS, VW=VW, BVW=BVW, SG=SG, BSG=BSG,
                                 KCMP=KCMP, BKCMP=BKCMP, VCMP=VCMP, BVCMP=BVCMP, rmat=rmat, cmask=cmask, causal4=causal4,
                                 wmask4=wmask4, ovx=ovx, ttab=ttab, pcadd=pcadd))
        sc.barrier()
        self.pstack.close()
        self.pstack = gstack
        self.end_phase()


Builder.phaseB = _phaseB


def _phaseB2(self, g, T):
    sc = self.sc
    QA4, BQA, KS, BKS, KW, BKW, VS, BVS, VW, BVW = (T[k] for k in ("QA4", "BQA", "KS", "BKS", "KW", "BKW", "VS", "BVS", "VW", "BVW"))
    SG, BSG, KCMP, BKCMP, VCMP, BVCMP = (T[k] for k in ("SG", "BSG", "KCMP", "BKCMP", "VCMP", "BVCMP"))
    RM, BRM = self.sb("RM", [128, 64, 128], BF16), Buf()
    sc.dma("pool", RM.rearrange("p a b -> p (a b)"), T["rmat"], writes=[BRM])
    CM, BCM = self.sb("CM", [128, 8, 512], BF16), Buf()
    sc.dma("pool", CM.rearrange("p a b -> p (a b)"), T["cmask"], writes=[BCM])
    CA4, BCA4 = self.sb("CA4", [128, 512], BF16), Buf()
    sc.dma("pool", CA4, T["causal4"], writes=[BCA4])
    WM4, BWM4 = self.sb("WM4", [128, 512], BF16), Buf()
    sc.dma("pool", WM4, T["wmask4"], writes=[BWM4])
    OVX, BOVX = self.sb("OVX", [128, 4, 130], BF16), Buf()
    sc.dma("pool", OVX.rearrange("p a b -> p (a b)"), T["ovx"], writes=[BOVX])
    TT, BTT = self.sb("TT", [128, 256], F32), Buf()
    sc.dma("sp", TT, T["ttab"], writes=[BTT])
    PC, BPC = self.sb("PC", [128, 128], F32), Buf()
    sc.dma("sp", PC, T["pcadd"], writes=[BPC])
    Er = [(self.sb("EB", [128, 512], BF16), Buf()) for _ in range(3)]
    E32 = [(self.sb("E16c", [128, 512], BF16), Buf()) for _ in range(4)]
    OB = [[(self.sb("OB", [128, 260], F32), Buf()) for _ in range(3)] for _ in range(2)]
    NST = [(self.sb("NST", [128, 512], BF16), Buf()) for _ in range(2)]
    dn, Bdn = self.sb("dn", [128, 4], F32), Buf()
    imp, Bimp = self.sb("imp", [128, 128], F32), Buf()
    rank, Brank = self.sb("rank", [128, 128], F32), Buf()
    r2, Br2 = self.sb("r2", [128, 128], F32), Buf()
    m8a, Bm8a = self.sb("m8a", [128, 8], F32), Buf()
    m8b, Bm8b = self.sb("m8b", [128, 8], F32), Buf()
    ns, Bns = self.sb("ns", [128, 128], BF16), Buf()
    dn3, Bdn3 = self.sb("dn3", [128, 12], F32), Buf()
    w3, Bw3 = self.sb("w3", [128, 12], F32), Buf()
    tmpo, Btmpo = self.sb("tmpo", [128, 64], F32), Buf()
    OBF, BOBF = self.sb("OBF", [128, 256], BF16), Buf()
    OBT = [(self.sb("OBT", [128, 2, 1024], BF16), Buf()) for _ in range(2)]
    QAf = QA4.rearrange("p n h q -> p n (h q)")
    if "b_nq" in self.debug:
        for o_, b_ in OBT:
            sc.op("pool", lambda e, o_=o_: e.memset(o_, 0.0), writes=[b_])
    def mk(n_):
        G = 2 * n_ + 1
        nct = (16 * n_ + 14) // 128 + 1
        c_ = [dict(br=0, n=n_, G=G, kt=ct, first=(ct == 0), last=(ct == nct - 1), nct=nct) for ct in range(nct)]
        wk = [kt for kt in range(G - 4, G + 1) if kt >= 0]
        w_ = [dict(br=2, n=n_, G=G, kt=kt, first=(kt == wk[0]), last=(kt == G)) for kt in wk]
        s_ = [dict(br=1, n=n_, G=G, kt=kt, first=(kt == 0), last=(kt == G)) for kt in range(G + 1)]
        return c_, w_, s_

    nq = self.debug.get("b_nq", 32)
    parts = [mk(n_) for n_ in range(nq)]
    steps = []
    for n_ in range(nq):
        steps += parts[n_][0]
        if n_ >= 1:
            steps += parts[n_ - 1][2]
        steps += parts[n_][1]
    if nq:
        steps += parts[nq - 1][2]
    rs = dict(s=0, e=0, o=0, nst=0)

    def stA(i):
        st = steps[i]
        br, n_, G, kt = st["br"], st["n"], st["G"], st["kt"]
        ss = rs["s"] % 2
        rs["s"] += 1
        st["ss"] = ss
        ps, Pps = self.bank(ss), self.Pbank[ss]
        q = QAf[:, n_, :]
        extras = []
        if br == 0:
            lhs, rl = KCMP[:, kt * 128:(kt + 1) * 128], [BKCMP]
            if st["last"]:
                extras.append((self.ident, CM[:, n_ % 8, :], [self.Bident, BCM]))
        else:
            Ksrc, Bk = (KS, BKS) if br == 1 else (KW, BKW)
            lhs, rl = Ksrc[:, kt * 128:(kt + 1) * 128], [Bk[kt // 4]]
            if br == 1:
                extras.append((RM[:, kt, :], NST[st_nst(n_)][0], [BRM, NST[st_nst(n_)][1]]))
            if kt == G:
                extras.append((self.ident, CA4, [self.Bident, BCA4]))
            if br == 2 and kt == G - 4:
                extras.append((self.ident, WM4, [self.Bident, BWM4]))
        sc.op("pe", lambda e: e.matmul(ps, lhsT=lhs, rhs=q, start=True, stop=(len(extras) == 0)), reads=rl + [BQA[n_ // 2]], writes=[Pps])
        for xi, (l, r, rd) in enumerate(extras):
            sc.op("pe", lambda e, l=l, r=r: e.matmul(ps, lhsT=l, rhs=r, start=False, stop=(xi == len(extras) - 1)), reads=rd, writes=[Pps])

    def st_nst(n_):
        return n_ % 2

    def stB(i):
        st = steps[i]
        br, kt = st["br"], st["kt"]
        ps, Pps = self.bank(st["ss"]), self.Pbank[st["ss"]]
        if br == 0:
            E, BE = E32[kt]
            st["Et"] = E32[kt]
        else:
            es = rs["e"] % 3
            rs["e"] += 1
            E, BE = Er[es]
            st["Et"] = Er[es]
        if kt == 0:
            col = 3 if br == 0 else 4
            bias = self.kvb[:, col:col + 1]
            sc.op("act", lambda e: e.activation(out=E, in_=ps, func=AF.Exp, scale=0.125, bias=bias), reads=[Pps, self.Bkvb], writes=[BE])
        else:
            sc.op("act", lambda e: e.activation(out=E, in_=ps, func=AF.Exp, scale=0.125), reads=[Pps], writes=[BE])

    def stC(i):
        st = steps[i]
        br, n_, G, kt = st["br"], st["n"], st["G"], st["kt"]
        E, BE = st["Et"]
        if st["first"]:
            rs["o"] += 1
            ob = 2 + rs["o"] % 2
            st["ob"] = ob
            sc.op("pe", lambda e: e.matmul(self.bank(ob)[:, 0:260], lhsT=self.ident, rhs=self.zeros[:, 0:260], start=True, stop=False),
                  reads=[self.Bident, self.Bzeros], writes=[self.Pbank[ob]])
        else:
            ob = steps[i - 1]["ob"]
            st["ob"] = ob
        pso, Po = self.bank(ob), self.Pbank[ob]
        if br == 0:
            V, Bv = VCMP[:, kt, 0:65], [BVCMP]
        elif br == 1:
            V, Bv = VS[:, kt, 0:65], [BVS[kt]]
        else:
            V, Bv = VW[:, kt, 0:65], [BVW[kt]]
        for h in range(4):
            sc.op("pe", lambda e, h=h: e.matmul(pso[:, h * 65:(h + 1) * 65], lhsT=E[:, h * 128:(h + 1) * 128], rhs=V,
                                                start=False, stop=(st["last"] and h == 3)),
                  reads=[BE] + Bv, writes=[Po])
        if not st["last"]:
            return
        par = n_ % 2
        O, BO = OB[par][br]
        sc.op("dve", lambda e: e.tensor_copy(out=O, in_=pso[:, 0:260]), reads=[Po], writes=[BO])
        if br == 0:
            nct = st["nct"]
            for h in range(4):
                bnk = 4 if h < 3 else 5
                off = (h % 3) * 129
                pi, Pi = self.bank(bnk), self.Pbank[bnk]
                for ct in range(nct):
                    sc.op("pe", lambda e, ct=ct: e.matmul(pi[:, off:off + 129], lhsT=E32[ct][0][:, h * 128:(h + 1) * 128], rhs=OVX[:, ct, 0:129],
                                                          start=(ct == 0), stop=(ct == nct - 1)),
                          reads=[E32[ct][1], BOVX], writes=[Pi])
            p4, p5 = self.bank(4), self.bank(5)
            sc.op("dve", lambda e: e.tensor_copy(out=dn[:, 0:3], in_=p4[:, 128:387:129]), reads=[self.Pbank[4]], writes=[Bdn])
            sc.op("dve", lambda e: e.tensor_copy(out=dn[:, 3:4], in_=p5[:, 128:129]), reads=[self.Pbank[5]], writes=[Bdn])
            sc.op("dve", lambda e: e.tensor_scalar(out=dn, in0=dn, scalar1=1e-30, scalar2=None, op0=ALU.max), reads=[Bdn], writes=[Bdn])
            sc.op("dve", lambda e: e.reciprocal(out=dn, in_=dn), reads=[Bdn], writes=[Bdn])
            sc.op("dve", lambda e: e.tensor_scalar(out=imp, in0=p4[:, 0:128], scalar1=dn[:, 0:1], scalar2=None, op0=ALU.mult),
                  reads=[self.Pbank[4], Bdn], writes=[Bimp])
            for h in range(1, 4):
                src = p4[:, h * 129:h * 129 + 128] if h < 3 else p5[:, 0:128]
                sc.op("dve", lambda e, h=h, src=src: e.scalar_tensor_tensor(out=imp, in0=src, scalar=dn[:, h:h + 1], in1=imp,
                                                                           op0=ALU.mult, op1=ALU.add),
                      reads=[self.Pbank[4 if h < 3 else 5], Bdn, Bimp], writes=[Bimp])
            sc.op("dve", lambda e: e.tensor_tensor(out=rank, in0=imp, in1=TT[:, 126 - 2 * G:126 - 2 * G + 128], op=ALU.add),
                  reads=[Bimp, BTT], writes=[Brank])
            sc.op("dve", lambda e: e.tensor_tensor(out=rank, in0=rank, in1=PC, op=ALU.add), reads=[Brank, BPC], writes=[Brank])
            sc.op("dve", lambda e: e.max(out=m8a, in_=rank), reads=[Brank], writes=[Bm8a])
            sc.op("dve", lambda e: e.match_replace(out=r2, in_to_replace=m8a, in_values=rank, imm_value=-3.0e38),
                  reads=[Brank, Bm8a], writes=[Br2])
            sc.op("dve", lambda e: e.max(out=m8b, in_=r2), reads=[Br2], writes=[Bm8b])
            sc.op("dve", lambda e: e.tensor_scalar(out=ns, in0=rank, scalar1=m8b[:, 7:8], scalar2=1.0, op0=ALU.is_ge, op1=ALU.subtract),
                  reads=[Brank, Bm8b], writes=[Bns])
            pT = self.bank(6).bitcast(BF16)[:, 0:128]
            sc.op("pe", lambda e: e.transpose(pT, ns, self.ident), reads=[Bns, self.Bident], writes=[self.Pbank[6]])
            N_, BN_ = NST[st_nst(n_)]
            for h in range(4):
                sc.op("dve", lambda e, h=h: e.tensor_copy(out=N_[:, h * 128:(h + 1) * 128], in_=pT), reads=[self.Pbank[6]], writes=[BN_])
            if "dbg_sel" in self.debug:
                dsel = self.dbg_sel
                sc.dma("sp", dsel[g, n_], N_[:, 0:128], reads=[BN_])
        if br == 1:
            Os = OB[par]
            for b_ in range(3):
                sc.op("dve", lambda e, b_=b_: e.tensor_copy(out=dn3[:, b_ * 4:(b_ + 1) * 4], in_=Os[b_][0][:, 64:260:65]),
                      reads=[Os[b_][1]], writes=[Bdn3])
            sc.op("dve", lambda e: e.tensor_scalar(out=dn3, in0=dn3, scalar1=1e-30, scalar2=None, op0=ALU.max), reads=[Bdn3], writes=[Bdn3])
            sc.op("dve", lambda e: e.reciprocal(out=dn3, in_=dn3), reads=[Bdn3], writes=[Bdn3])
            sgv = SG[:, n_, :].rearrange("p (h b) -> p b h", b=3)
            sc.op("dve", lambda e: e.tensor_tensor(out=w3.rearrange("p (b h) -> p b h", b=3), in0=dn3.rearrange("p (b h) -> p b h", b=3),
                                                   in1=sgv, op=ALU.mult),
                  reads=[Bdn3, BSG], writes=[Bw3])
            for h in range(4):
                sl = slice(h * 65, h * 65 + 64)
                sc.op("dve", lambda e: e.tensor_scalar(out=tmpo, in0=Os[0][0][:, sl], scalar1=w3[:, h:h + 1], scalar2=None, op0=ALU.mult),
                      reads=[Os[0][1], Bw3], writes=[Btmpo])
                sc.op("dve", lambda e: e.scalar_tensor_tensor(out=tmpo, in0=Os[1][0][:, sl], scalar=w3[:, 4 + h:5 + h], in1=tmpo,
                                                              op0=ALU.mult, op1=ALU.add),
                      reads=[Os[1][1], Bw3, Btmpo], writes=[Btmpo])
                sc.op("dve", lambda e: e.scalar_tensor_tensor(out=OBF[:, h * 64:(h + 1) * 64], in0=Os[2][0][:, sl], scalar=w3[:, 8 + h:9 + h],
                                                              in1=tmpo, op0=ALU.mult, op1=ALU.add),
                      reads=[Os[2][1], Bw3, Btmpo], writes=[BOBF])
            stg, Bstg = OBT[(n_ // 8) % 2]
            pT7 = self.bank(7).bitcast(BF16)
            for ch in range(2):
                sc.op("pe", lambda e, ch=ch: e.transpose(pT7[:, ch * 128:(ch + 1) * 128], OBF[:, ch * 128:(ch + 1) * 128], self.ident),
                      reads=[BOBF, self.Bident], writes=[self.Pbank[7]])
            sc.op("dve", lambda e: e.tensor_copy(out=stg[:, :, (n_ % 8) * 128:(n_ % 8 + 1) * 128],
                                                 in_=pT7[:, 0:256].rearrange("p (c q) -> p c q", c=2)),
                  reads=[self.Pbank[7]], writes=[Bstg])
            if n_ % 8 == 7 or i == len(steps) - 1:
                n0 = (n_ // 8) * 8
                for ch in range(2):
                    r0 = g * 256 + ch * 128
                    sc.dma("sp", self.obT[r0:r0 + 128, n0 * 128:n0 * 128 + 1024], stg[:, ch, :], reads=[Bstg], writes=[self.BobT])

    pipeline(len(steps), [stA, stB, stC])


def _tail_chunks():
    ch = []
    for co in range(8):
        ch += [("ga", co, 8), ("gb", co, 8), ("ua", co, 4), ("ub", co, 4)]
    for co in range(8):
        ch.append(("out", co, 8))
    for hc in range(32):
        ch.append(("w1", hc, 8))
    for co in range(8):
        ch.append(("w2", co, 32))
    for co in range(8):
        ch += [("pg", co, 8), ("ple", co, 2)]
    offs, o = [], 0
    for (_, _, kc) in ch:
        offs.append(o)
        o += kc * 16384
    return ch, offs, o


TAIL_CH, TAIL_OFF, TAIL_N = _tail_chunks()


def _cast_tail_weights(self):
    sc = self.sc
    wt = self.inp("wtail", [TAIL_N // 1024, 1024])
    self.wtb = self.scratch("wtb", [TAIL_N // 1024, 1024], BF16)
    rows = TAIL_N // 1024
    nseg = 8
    self.Bwtb = Buf()
    per = rows // nseg
    self._cast_segs = []
    for i in range(nseg):
        r0, r1 = i * per, (rows if i == nseg - 1 else (i + 1) * per)
        self._cast_segs.append((self.wtb[r0:r1, :], wt[r0:r1, :]))


def _cast_some(self, n):
    for _ in range(n):
        if self._cast_segs:
            o, i_ = self._cast_segs.pop(0)
            self.sc.dma("pool", o, i_, writes=[self.Bwtb])


Builder.cast_some = _cast_some


Builder.cast_tail_weights = _cast_tail_weights


def _phaseT(self, tiles=(0, 1, 2, 3)):
    sc = self.sc
    self.begin_phase()
    pT = self.inp("pT", [256, SO])
    outT = self.nc.dram_tensor("outT", [D, SO], F32, kind="ExternalOutput").ap()
    outr = outT.rearrange("(c p) t -> p c t", p=128)
    wflat = self.wtb.rearrange("r c -> (r c)")
    H, BH = self.sb("tH", [128, 8, 1024], F32), Buf()
    NB, BNB = self.sb("tNB", [128, 8, 1024], BF16), Buf()
    MIXf = self.sb("tMIX", [128, 8192], BF16)
    MIX, BMIX = MIXf.rearrange("p (c t) -> p c t", c=8), Buf()
    SQ = MIXf.bitcast(F32).rearrange("p (c t) -> p c t", c=8)
    HIDf = self.sb("tHID", [128, 32 * 1024], BF16)
    HID, BHID = HIDf.rearrange("p (c t) -> p c t", c=32), Buf()
    OAB, BOAB = HID[:, 0:8, :], Buf()
    NBX, BNBX = HID[:, 8:16, :], Buf()
    OUTB = HIDf.bitcast(F32)[:, 8192:16384].rearrange("p (c t) -> p c t", c=8)
    SGA = [(self.sb("tSG", [128, 1024], F32), Buf()) for _ in range(2)]
    TMP = [(self.sb("tTMP", [128, 1024], F32), Buf()) for _ in range(2)]
    RL = [(self.sb("tRL", [128, 512], BF16), Buf()) for _ in range(2)]
    PTt, BPT = self.sb("tPT", [128, 2, 1024], BF16), Buf()
    WR = [(self.sb("tW", [128, 32 * 128], BF16), Buf()) for _ in range(3)]
    rs, Brs = self.sb("trs", [128, 512], F32), Buf()
    ri, Bri = self.sb("tri", [128, 512], F32), Buf()
    ACC, BACC = self.sb("tACC", [128, 1024], F32), [Buf(), Buf()]
    SQC, BSQC = self.sb("tSQC", [128, 1024], F32), [Buf(), Buf()]
    hsl = lambda hh: slice(hh * 512, (hh + 1) * 512)

    def stat_update(co, hh, Bg):
        if co == 0:
            sc.op("act", lambda e: e.activation(out=ACC[:, hsl(hh)], in_=H[:, co, hsl(hh)], func=AF.Square),
                  reads=[Bg], writes=[BACC[hh]])
        else:
            sc.op("act", lambda e: e.activation(out=SQC[:, hsl(hh)], in_=H[:, co, hsl(hh)], func=AF.Square),
                  reads=[Bg], writes=[BSQC[hh]])
            sc.op("pool", lambda e: e.tensor_tensor(out=ACC[:, hsl(hh)], in0=ACC[:, hsl(hh)], in1=SQC[:, hsl(hh)], op=ALU.add),
                  reads=[BSQC[hh]], writes=[BACC[hh]])

    def stat_update_pool(co, Bg):
        if co == 0:
            sc.op("pool", lambda e: e.tensor_tensor(out=ACC, in0=H[:, co, :], in1=H[:, co, :], op=ALU.mult),
                  reads=[Bg], writes=BACC)
        else:
            sc.op("pool", lambda e: e.tensor_tensor(out=SQC, in0=H[:, co, :], in1=H[:, co, :], op=ALU.mult),
                  reads=[Bg], writes=BSQC)
            sc.op("pool", lambda e: e.tensor_tensor(out=ACC, in0=ACC, in1=SQC, op=ALU.add), reads=BSQC, writes=BACC)

    def norm_finish(out_ap, Bout, gidx):
        for hh in range(2):
            ps, Pps = self.bank(6 + hh), self.Pbank[6 + hh]
            sc.op("pe", lambda e: e.matmul(ps, lhsT=self.ones32, rhs=ACC[:, hsl(hh)], start=True, stop=True),
                  reads=[self.Bones32, BACC[hh]], writes=[Pps])
            sc.op("act", lambda e: e.activation(out=rs, in_=ps, func=AF.Sqrt, scale=1.0 / D, bias=self.epsc[:, 0:1]),
                  reads=[Pps, self.Beps], writes=[Brs])
            sc.op("dve", lambda e: e.reciprocal(out=ri, in_=rs), reads=[Brs], writes=[Bri])
            for c in range(8):
                sc.op("dve", lambda e, c=c: e.scalar_tensor_tensor(
                    out=out_ap[:, c, hsl(hh)], in0=H[:, c, hsl(hh)], scalar=self.gains[:, gidx, c:c + 1],
                    in1=ri, op0=ALU.mult, op1=ALU.mult),
                    reads=[BH, self.Bgains, Bri], writes=[Bout])
    nch = len(TAIL_CH)
    state = dict(issued=0, bank=0, sg=0, tmp=0, rl=0)
    total = nch * len(tiles)

    def issue(gi):
        ci = gi % nch
        kc = TAIL_CH[ci][2]
        w, bw = WR[gi % 3]
        src = wflat[TAIL_OFF[ci]:TAIL_OFF[ci] + kc * 16384].rearrange("(p x) -> p x", p=128)
        sc.dma("sp", w[:, 0:kc * 128], src, reads=[self.Bwtb], writes=[bw])

    def wchunk(gi):
        while state["issued"] < min(gi + 3, total):
            issue(state["issued"])
            state["issued"] += 1
        w, bw = WR[gi % 3]
        kc = TAIL_CH[gi % nch][2]
        return w[:, 0:kc * 128].rearrange("p (k j) -> p k j", j=128), bw

    def nbank():
        state["bank"] = (state["bank"] + 1) % 6
        return state["bank"]

    def mm_group(w, bw, kc, rhs_fn, rbufs, hh):
        b = nbank()
        ps, Pps = self.bank(b), self.Pbank[b]
        for k in range(kc):
            sc.op("pe", lambda e, k=k: e.matmul(ps, lhsT=w[:, k, :], rhs=rhs_fn(k, hh), start=(k == 0), stop=(k == kc - 1)),
                  reads=[bw] + rbufs, writes=[Pps])
        return ps, Pps

    def load_nb_oab(tt):
        c0 = tt * 1024
        nsrc = self.nr[:, :, 2048 * tt:2048 * tt + 2048].rearrange("p c (b x) -> p c b x", b=8)[:, :, :, 128:256]
        for c_ in range(8):
            sc.dma("sp", NBX[:, c_, :].rearrange("p (b x) -> p b x", b=8), nsrc[:, c_], reads=self.BnT, writes=[BNBX])

    def load_oab(tt):
        c0 = tt * 1024
        sc.dma("sp", OAB[:, 0:4, :], self.oaT.rearrange("(c p) t -> p c t", p=128)[:, :, c0:c0 + 1024], reads=[self.BoaT], writes=[BOAB])
        sc.dma("sp", OAB[:, 4:8, :], self.obT.rearrange("(c p) t -> p c t", p=128)[:, :, c0:c0 + 1024], reads=[self.BobT], writes=[BOAB])

    def load_h(tt):
        c0 = tt * 1024
        xsrc = self.xr[:, :, 2048 * tt:2048 * tt + 2048].rearrange("p c (b x) -> p c b x", b=8)[:, :, :, 128:256]
        for c_ in range(8):
            sc.dma("sp", H[:, c_, :].rearrange("p (b x) -> p b x", b=8), xsrc[:, c_], writes=[BH])
        sc.dma("pool", PTt, pT.rearrange("(c p) t -> p c t", p=128)[:, :, c0:c0 + 1024], writes=[BPT])

    gi = 0
    for ti, tt in enumerate(tiles):
        nxt = tiles[ti + 1] if ti + 1 < len(tiles) else None
        c0 = tt * 1024
        hs = lambda hh: slice(hh * 512, (hh + 1) * 512)
        if ti == 0:
            load_oab(tt)
            load_nb_oab(tt)
            load_h(tt)
        for co in range(8):
            wga, bga = wchunk(gi); gi += 1
            sga, Bsga = SGA[0]
            for hh in range(2):
                ps, Pps = mm_group(wga, bga, 8, lambda k, hh: NBX[:, k, hs(hh)], [BNBX], hh)
                sc.op("act", lambda e: e.activation(out=sga[:, hs(hh)], in_=ps, func=AF.Sigmoid), reads=[Pps], writes=[Bsga])
            wgb, bgb = wchunk(gi); gi += 1
            sgb, Bsgb = SGA[1]
            for hh in range(2):
                ps, Pps = mm_group(wgb, bgb, 8, lambda k, hh: NBX[:, k, hs(hh)], [BNBX], hh)
                sc.op("act", lambda e: e.activation(out=sgb[:, hs(hh)], in_=ps, func=AF.Sigmoid), reads=[Pps], writes=[Bsgb])
            wua, bua = wchunk(gi); gi += 1
            ta, Bta = TMP[0]
            for hh in range(2):
                ps, Pps = mm_group(wua, bua, 4, lambda k, hh: OAB[:, k, hs(hh)], [BOAB], hh)
                sc.op("dve", lambda e: e.tensor_tensor(out=ta[:, hs(hh)], in0=ps, in1=sga[:, hs(hh)], op=ALU.mult),
                      reads=[Pps, Bsga], writes=[Bta])
            wub, bub = wchunk(gi); gi += 1
            tb_, Btb = TMP[1]
            for hh in range(2):
                ps, Pps = mm_group(wub, bub, 4, lambda k, hh: OAB[:, 4 + k, hs(hh)], [BOAB], hh)
                sc.op("dve", lambda e: e.tensor_tensor(out=tb_[:, hs(hh)], in0=ps, in1=sgb[:, hs(hh)], op=ALU.mult),
                      reads=[Pps, Bsgb], writes=[Btb])
            sc.op("pool", lambda e: e.tensor_tensor(out=MIX[:, co, :], in0=ta, in1=tb_, op=ALU.add), reads=[Bta, Btb], writes=[BMIX])
        for co in range(8):
            w, bw = wchunk(gi); gi += 1
            for hh in range(2):
                ps, Pps = mm_group(w, bw, 8, lambda k, hh: MIX[:, k, hs(hh)], [BMIX], hh)
                Bg = Buf()
                sc.op("dve", lambda e: e.tensor_tensor(out=H[:, co, hs(hh)], in0=H[:, co, hs(hh)], in1=ps, op=ALU.add),
                      reads=[Pps, BH], writes=[BH, Bg])
                stat_update(co, hh, Bg)
        norm_finish(NB, BNB, 1)
        for hc in range(32):
            w, bw = wchunk(gi); gi += 1
            for hh in range(2):
                ps, Pps = mm_group(w, bw, 8, lambda k, hh: NB[:, k, hs(hh)], [BNB], hh)
                r_, Br = RL[state["rl"] % 2]
                state["rl"] += 1
                sc.op("act", lambda e: e.activation(out=r_, in_=ps, func=AF.Relu), reads=[Pps], writes=[Br])
                sc.op("dve", lambda e: e.scalar_tensor_tensor(out=HID[:, hc, hs(hh)], in0=ps, scalar=0.0, in1=r_, op0=ALU.max, op1=ALU.mult),
                      reads=[Pps, Br], writes=[BOAB if hc < 8 else (BNBX if hc < 16 else BHID)])
        for co in range(8):
            w, bw = wchunk(gi); gi += 1
            for hh in range(2):
                ps, Pps = mm_group(w, bw, 32, lambda k, hh: HID[:, k, hs(hh)], [BHID, BOAB, BNBX], hh)
                Bg = Buf()
                sc.op("dve", lambda e: e.tensor_tensor(out=H[:, co, hs(hh)], in0=H[:, co, hs(hh)], in1=ps, op=ALU.add),
                      reads=[Pps, BH], writes=[BH, Bg])
                stat_update(co, hh, Bg)
        if nxt is not None:
            load_oab(nxt)
            load_nb_oab(nxt)
        norm_finish(NB, BNB, 2)
        for co in range(8):
            wpg, bpg = wchunk(gi); gi += 1
            sga, Bsga = SGA[co % 2]
            for hh in range(2):
                ps, Pps = mm_group(wpg, bpg, 8, lambda k, hh: NB[:, k, hs(hh)], [BNB], hh)
                sc.op("act", lambda e: e.activation(out=sga[:, hs(hh)], in_=ps, func=AF.Sigmoid), reads=[Pps], writes=[Bsga])
            wpl, bpl = wchunk(gi); gi += 1
            ta, Bta = TMP[co % 2]
            for hh in range(2):
                ps, Pps = mm_group(wpl, bpl, 2, lambda k, hh: PTt[:, k, hs(hh)], [BPT], hh)
                sc.op("dve", lambda e: e.tensor_tensor(out=ta[:, hs(hh)], in0=ps, in1=sga[:, hs(hh)], op=ALU.mult),
                      reads=[Pps, Bsga], writes=[Bta])
            Bg = Buf()
            sc.op("dve", lambda e: e.tensor_tensor(out=H[:, co, :], in0=H[:, co, :], in1=ta, op=ALU.add), reads=[Bta, BH], writes=[BH, Bg])
            stat_update_pool(co, Bg)
        norm_finish(OUTB, BHID, 3)
        if nxt is not None:
            load_h(nxt)
        sc.dma("sp", outr[:, :, c0:c0 + 1024], OUTB, reads=[BHID], is_output=True)
    self.end_phase()


Builder.phaseT = _phaseT


def _own_tokens(p):
    gb = np.arange(32) * 2 + (1 if p == 1 else 0)
    return (gb[:, None] * 128 + np.arange(128)[None, :]).reshape(-1)


_PROGRAM = {}


def kernel(**inputs):
    inputs = {k: np.asarray(v) for k, v in inputs.items()}
    x = np.asarray(inputs["x"], np.float32)
    pin = np.asarray(inputs["p"], np.float32)[0]
    w = _prep_weights(inputs)
    c = _consts_common()
    if "B" not in _PROGRAM:
        import os
        B = build_program(upto=os.environ.get("KERNEL_UPTO", "all"))
        B.sc.finish()
        _PROGRAM["B"] = B
    B = _PROGRAM["B"]
    maps = []
    for core in range(8):
        b, p = core // 2, core % 2
        m = dict(w)
        m.update(c)
        m["kvb"] = _kvb(p)
        m["pcadd"] = _pcadd(p)
        m["xT"] = _virtual_xT(x[b], p)
        m["pT"] = np.ascontiguousarray(pin[b][_own_tokens(p)].T)
        maps.append({k: np.ascontiguousarray(v, dtype=np.float32) for k, v in m.items() if k in B.din})
    res = run_bass_kernel_spmd(B.nc, maps, core_ids=list(range(8)))
    out = np.zeros((4, S, D), np.float32)
    for core in range(8):
        b, p = core // 2, core % 2
        if "outT" in res.results[core]:
            out[b, _own_tokens(p)] = np.asarray(res.results[core]["outT"], np.float32).T
    return out
```
